# Optimizing a Trainium2 kernel written in Bass

```python
import math
import jax, jax.numpy as jnp
from jax import lax
import numpy as np

D_MODEL = 1024
BATCH = 4
SEQ = 4096
DEPTH = 4

N_MIXERS = 2
CONV_EXPAND = 2
CONV_WIDTH = D_MODEL * CONV_EXPAND
CONV_K = 3
HEAD_DIM = 64
HEADS_PER_GROUP = D_MODEL // HEAD_DIM
DILATED_GROUPS = ((128, 1), (512, 4), (2048, 16))
N_GROUPS = len(DILATED_GROUPS)
ATTN_WIDTH = HEADS_PER_GROUP * HEAD_DIM
QKV_COLS = N_GROUPS * 3 * ATTN_WIDTH
BLOCK = 128
N_BUCKETS = 32
MAX_DISTANCE = 2048
EPS = 1e-6
N_CONV_LAYERS = (DEPTH + 1) // 2
N_ATTN_LAYERS = DEPTH // 2

kernel_name = "hybrid_shortconv_dilated_attn_trunk"


def rms_norm(x, g):
    xf = x.astype(jnp.float32)
    y = xf * lax.rsqrt(jnp.mean(xf * xf, axis=-1, keepdims=True) + EPS)
    return (y * g.astype(jnp.float32)).astype(x.dtype)


def t5_bucket(dist):
    max_exact = N_BUCKETS // 2
    d = jnp.maximum(dist, 1).astype(jnp.float32)
    large = max_exact + (jnp.log(d / max_exact) / math.log(MAX_DISTANCE / max_exact)
                         * (N_BUCKETS - max_exact)).astype(jnp.int32)
    large = jnp.minimum(large, N_BUCKETS - 1)
    return jnp.where(dist < max_exact, dist, large)


def short_conv_mixer(h, w_in, w_conv, w_out):
    S = h.shape[1]
    proj = h @ w_in
    b_gate, c_gate, u, z = jnp.split(proj, 4, axis=-1)
    v = c_gate * u
    vp = jnp.pad(v, ((0, 0), (CONV_K - 1, 0), (0, 0)))
    conv = sum(w_conv[k] * vp[:, k:k + S] for k in range(CONV_K))
    y = b_gate * conv * jax.nn.silu(z)
    return y @ w_out


def dilated_group_attention(q, k, v, bias_tab, window, dilation):
    B, S, H, Dh = q.shape
    L = S // dilation
    nb = -(-L // BLOCK)
    Lp = nb * BLOCK
    span = window // dilation

    def to_sub(t):
        t = t.reshape(B, L, dilation, H, Dh).transpose(0, 2, 3, 1, 4)
        t = jnp.pad(t, ((0, 0), (0, 0), (0, 0), (0, Lp - L), (0, 0)))
        return t.reshape(B, dilation, H, nb, BLOCK, Dh)

    def band(t):
        prev = jnp.pad(t, ((0, 0), (0, 0), (0, 0), (1, 0), (0, 0), (0, 0)))[:, :, :, :-1]
        return jnp.concatenate([prev, t], axis=4)

    qs = to_sub(q)
    kb = band(to_sub(k))
    vb = band(to_sub(v))
    logits = jnp.einsum('brhnqd,brhnkd->brhnqk', qs, kb).astype(jnp.float32) * (HEAD_DIM ** -0.5)

    a = jnp.arange(BLOCK)[:, None]
    kk = jnp.arange(2 * BLOCK)[None, :]
    step = BLOCK + a - kk
    blk = jnp.arange(nb)[:, None, None]
    key_pos = (blk - 1) * BLOCK + kk[None]
    valid = (step >= 0) & (step <= span) & (key_pos >= 0)
    bucket = t5_bucket(jnp.clip(step, 0, span) * dilation)
    bias = bias_tab.astype(jnp.float32)[bucket].transpose(2, 0, 1)[:, None]

    logits = jnp.where(valid, logits + bias, -jnp.inf)
    lse = jax.nn.logsumexp(logits, axis=-1)
    probs = jnp.exp(logits - lse[..., None])
    out = jnp.einsum('brhnqk,brhnkd->brhnqd', probs.astype(v.dtype), vb)

    out = out.reshape(B, dilation, H, Lp, Dh)[:, :, :, :L]
    out = out.transpose(0, 3, 1, 2, 4).reshape(B, S, H, Dh)
    lse = lse.reshape(B, dilation, H, Lp)[:, :, :, :L]
    lse = lse.transpose(0, 3, 1, 2).reshape(B, S, H)
    return out, lse


def dilated_attention_mixer(h, w_in, q_gain, k_gain, w_out, rel_bias):
    B, S, _ = h.shape
    proj = h @ w_in
    qkv = proj[..., :QKV_COLS].reshape(B, S, N_GROUPS, 3, HEADS_PER_GROUP, HEAD_DIM)
    z = proj[..., QKV_COLS:]
    outs, lses = [], []
    for g, (window, dilation) in enumerate(DILATED_GROUPS):
        q = rms_norm(qkv[:, :, g, 0], q_gain[g])
        k = rms_norm(qkv[:, :, g, 1], k_gain[g])
        bias_g = rel_bias[:, g * HEADS_PER_GROUP:(g + 1) * HEADS_PER_GROUP]
        o, l = dilated_group_attention(q, k, qkv[:, :, g, 2], bias_g, window, dilation)
        outs.append(o)
        lses.append(l)
    alpha = jax.nn.softmax(jnp.stack(lses, axis=0), axis=0)
    o = jnp.sum(alpha[..., None] * jnp.stack(outs, axis=0).astype(jnp.float32), axis=0)
    y = o.reshape(B, S, ATTN_WIDTH).astype(h.dtype) * jax.nn.silu(z)
    return y @ w_out


def setup_inputs(seed: int = 0) -> dict:
    key = jax.random.key(seed)
    ks = jax.random.split(key, 12)
    f32 = jnp.float32
    nc, na = N_CONV_LAYERS, N_ATTN_LAYERS
    x = jax.random.normal(ks[0], (BATCH, SEQ, D_MODEL), f32)
    conv_norm = 1.0 + 0.1 * jax.random.normal(ks[1], (nc, D_MODEL), f32)
    conv_w_in = jax.random.normal(ks[2], (nc, D_MODEL, 4 * CONV_WIDTH), f32) * D_MODEL ** -0.5
    conv_w = jax.random.normal(ks[3], (nc, CONV_K, CONV_WIDTH), f32) * CONV_K ** -0.5
    conv_w_out = jax.random.normal(ks[4], (nc, CONV_WIDTH, D_MODEL), f32) * CONV_WIDTH ** -0.5
    attn_norm = 1.0 + 0.1 * jax.random.normal(ks[5], (na, D_MODEL), f32)
    attn_w_in = jax.random.normal(ks[6], (na, D_MODEL, QKV_COLS + ATTN_WIDTH), f32) * D_MODEL ** -0.5
    attn_q_gain = 1.0 + 0.1 * jax.random.normal(ks[7], (na, N_GROUPS, HEAD_DIM), f32)
    attn_k_gain = 1.0 + 0.1 * jax.random.normal(ks[8], (na, N_GROUPS, HEAD_DIM), f32)
    attn_w_out = jax.random.normal(ks[9], (na, ATTN_WIDTH, D_MODEL), f32) * ATTN_WIDTH ** -0.5
    rel_bias = 0.5 * jax.random.normal(ks[10], (N_BUCKETS, N_GROUPS * HEADS_PER_GROUP), f32)
    return {"x": x, "conv_norm": conv_norm, "conv_w_in": conv_w_in, "conv_w": conv_w,
            "conv_w_out": conv_w_out, "attn_norm": attn_norm, "attn_w_in": attn_w_in,
            "attn_q_gain": attn_q_gain, "attn_k_gain": attn_k_gain, "attn_w_out": attn_w_out,
            "rel_bias": rel_bias}


def reference(x, conv_norm, conv_w_in, conv_w, conv_w_out, attn_norm, attn_w_in,
              attn_q_gain, attn_k_gain, attn_w_out, rel_bias):
    for i in range(DEPTH):
        j = i // N_MIXERS
        if i % N_MIXERS == 0:
            h = rms_norm(x, conv_norm[j])
            x = x + short_conv_mixer(h, conv_w_in[j], conv_w[j], conv_w_out[j])
        else:
            h = rms_norm(x, attn_norm[j])
            x = x + dilated_attention_mixer(h, attn_w_in[j], attn_q_gain[j], attn_k_gain[j],
                                            attn_w_out[j], rel_bias)
    return x
```

```python
import math
import numpy as np
import concourse.bass as bass
import concourse.mybir as mybir
from concourse.bass_utils import run_bass_kernel_spmd

F32 = mybir.dt.float32
BF16 = mybir.dt.bfloat16
AF = mybir.ActivationFunctionType
ALU = mybir.AluOpType

NCORES = 8
T = 2048
NT = T // 128
D_MODEL = 1024
EPS = 1e-6
GROUP_DIL = (1, 4, 16)
FUSED = True
import os
NOHALO = bool(int(os.environ.get('NOHALO', '0')))
FORCE_NEGF = bool(int(os.environ.get('FORCE_NEGF', '0')))
PAIRS = [[0, 1], [2, 3], [4, 5], [6, 7]]
FPER = 382


class Buf:
    __slots__ = ("w", "r", "name")

    def __init__(self, name=""):
        self.w = None
        self.r = {}
        self.name = name


class Prog:
    ENG = ("pe", "act", "dve", "pool", "sp")

    def __init__(self, nc, stack):
        self.nc = nc
        self.ops = {e: [] for e in self.ENG}
        self.sem = {}
        self.cnt = {}
        self.waited = {e: {} for e in self.ENG}
        for e in self.ENG:
            self.sem[e] = stack.enter_context(nc.semaphore("s_" + e))
            self.cnt[e] = 0
        self.ndma = 24
        self.dsem = []
        for q in ("sp", "pool"):
            for i in range(self.ndma):
                nm = "d_%s_%d" % (q, i)
                self.sem[nm] = stack.enter_context(nc.semaphore(nm))
                self.cnt[nm] = 0
        self.dnext = {"sp": 0, "pool": 0}
        self.sem["cc"] = stack.enter_context(nc.semaphore("s_cc"))
        self.cnt["cc"] = 0

    def _waits(self, eng, rd, wr, extra):
        need = {}

        def add(t):
            if t is None:
                return
            s, v = t
            if s == eng and v > self.cnt[eng]:
                return
            if need.get(s, 0) < v:
                need[s] = v
        for b in rd:
            add(b.w)
        for b in wr:
            add(b.w)
            for t in b.r.values():
                add(t)
        for t in extra:
            add(t)
        out = []
        wd = self.waited[eng]
        for s, v in need.items():
            if wd.get(s, 0) < v:
                wd[s] = v
                out.append((s, v))
        return out

    def _commit(self, ticket, rd, wr):
        for b in rd:
            b.r[ticket[0]] = ticket
        for b in wr:
            b.w = ticket
            b.r = {}

    def op(self, eng, fn, rd=(), wr=(), extra=(), sig=True):
        waits = self._waits(eng, rd, wr, extra)
        if sig:
            self.cnt[eng] += 1
            tk = (eng, self.cnt[eng])
            self.ops[eng].append((waits, fn, (eng, 1)))
        else:
            tk = (eng, self.cnt[eng] + 1)
            self.ops[eng].append((waits, fn, None))
        self._commit(tk, rd, wr)
        return tk

    def dma(self, q, out, in_, rd=(), wr=(), extra=(), **kw):
        i = self.dnext[q]
        self.dnext[q] = (i + 1) % self.ndma
        nm = "d_%s_%d" % (q, i)
        prev = (nm, self.cnt[nm]) if self.cnt[nm] > 0 else None
        waits = self._waits(q, rd, wr, list(extra) + [prev])
        self.cnt[nm] += 16
        tk = (nm, self.cnt[nm])
        self.ops[q].append((waits, lambda e: e.dma_start(out=out, in_=in_, **kw), (nm, 16)))
        self._commit(tk, rd, wr)
        return tk

    def collective(self, fn, rd=(), wr=()):
        waits = self._waits("pool", rd, wr, [("cc", self.cnt["cc"])] if self.cnt["cc"] else [])
        self.cnt["cc"] += 1
        tk = ("cc", self.cnt["cc"])
        self.ops["pool"].append((waits, fn, ("cc", 1)))
        self._commit(tk, rd, wr)
        return tk

    def barrier(self):
        tk = [(s, c) for s, c in self.cnt.items() if c > 0]
        for e in self.ENG:
            self.wait_all(e, tk)

    def mm(self, out, lhsT, rhs, start, stop, rd, wr, sig=True):
        return self.op("pe", lambda e: e.matmul(out, lhsT, rhs, start=start, stop=stop), rd=rd, wr=wr, sig=sig)

    def tr(self, out, in_, ident, rd, wr, sig=True):
        return self.op("pe", lambda e: e.transpose(out, in_, ident), rd=rd, wr=wr, sig=sig)

    def act(self, out, in_, func, rd, wr, **kw):
        return self.op("act", lambda e: e.activation(out=out, in_=in_, func=func, **kw), rd=rd, wr=wr)

    def acopy(self, out, in_, rd, wr):
        return self.op("act", lambda e: e.copy(out=out, in_=in_), rd=rd, wr=wr)

    def tt(self, eng, out, in0, in1, op, rd, wr):
        return self.op(eng, lambda e: e.tensor_tensor(out=out, in0=in0, in1=in1, op=op), rd=rd, wr=wr)

    def stt(self, eng, out, in0, scalar, in1, op0, op1, rd, wr):
        return self.op(eng, lambda e: e.scalar_tensor_tensor(out=out, in0=in0, scalar=scalar, in1=in1, op0=op0, op1=op1), rd=rd, wr=wr)

    def ts(self, eng, out, in0, scalar1, op0, rd, wr):
        return self.op(eng, lambda e: e.tensor_scalar(out=out, in0=in0, scalar1=scalar1, scalar2=None, op0=op0), rd=rd, wr=wr)

    def cp(self, eng, out, in_, rd, wr):
        return self.op(eng, lambda e: e.tensor_copy(out=out, in_=in_), rd=rd, wr=wr)

    def recip(self, out, in_, rd, wr):
        return self.op("dve", lambda e: e.reciprocal(out=out, in_=in_), rd=rd, wr=wr)

    def memset(self, eng, ap, val, wr):
        return self.op(eng, lambda e: e.memset(ap, val), wr=wr)

    def wait_all(self, eng, tickets):
        waits = self._waits(eng, (), (), tickets)
        self.ops[eng].append((waits, None, None))

    def emit(self, block):
        sem = self.sem

        def mk(eng):
            lst = self.ops[eng]

            def body(e):
                for waits, fn, inc in lst:
                    for s, v in waits:
                        e.wait_ge(sem[s], v)
                    if fn is None:
                        continue
                    r = fn(e)
                    if inc is not None:
                        r.then_inc(sem[inc[0]], inc[1])
            return body
        block.tensor(mk("pe"))
        block.scalar(mk("act"))
        block.vector(mk("dve"))
        block.gpsimd(mk("pool"))
        block.sync(mk("sp"))


def sb_ap(t, off, dims):
    base = t[:]
    p = base.ap[0]
    return bass.AP(t, off, [[p[0], p[1]]] + [list(d) for d in dims])


def sb_ap_p(t, p0, np_, off, dims):
    base = t[p0:p0 + np_]
    p = base.ap[0]
    return bass.AP(t, base.offset + off, [[p[0], p[1]]] + [list(d) for d in dims])


def _t5_bucket_np(dist):
    dist = np.asarray(dist, dtype=np.int64)
    d = np.maximum(dist, 1).astype(np.float32)
    large = 16 + (np.log(d / np.float32(16)) / np.float32(math.log(2048 / 16)) * np.float32(16)).astype(np.int32)
    large = np.minimum(large, 31)
    return np.where(dist < 16, dist, large).astype(np.int64)


def bias_onehot():
    oh = np.zeros((64, 3, FPER), np.float32)
    for g, dil in enumerate(GROUP_DIL):
        for j in range(FPER):
            step = j - 127
            if 0 <= step <= 128:
                b = int(_t5_bucket_np(np.array([step * dil]))[0])
                oh[b, g, j] = 1.0
            else:
                oh[32, g, j] = -30000.0
    return oh


def build(layers):
    from contextlib import ExitStack
    nc = bass.Bass("TRN2", target_bir_lowering=False)
    dt = nc.dram_tensor
    x_in = dt("x", [T, D_MODEL], F32, kind="ExternalInput").ap()
    xh_in = dt("xh", [2, D_MODEL], F32, kind="ExternalInput").ap()
    flag_in = dt("flag", [128, 1], F32, kind="ExternalInput").ap()
    ident_in = dt("ident", [128, 128], F32, kind="ExternalInput").ap()
    onesbd_in = dt("onesbd", [128, 128], F32, kind="ExternalInput").ap()
    oh_in = dt("oh", [64, 3, FPER], F32, kind="ExternalInput").ap()
    conv_norm = dt("conv_norm", [2, 1024], F32, kind="ExternalInput").ap()
    conv_w_in = dt("conv_w_in", [2, 1024, 8192], F32, kind="ExternalInput").ap()
    conv_w = dt("conv_w", [2, 3, 2048], F32, kind="ExternalInput").ap()
    conv_w_out = dt("conv_w_out", [2, 2048, 1024], F32, kind="ExternalInput").ap()
    attn_norm = dt("attn_norm", [2, 1024], F32, kind="ExternalInput").ap()
    attn_w_in = dt("attn_w_in", [2, 1024, 10240], F32, kind="ExternalInput").ap()
    attn_q_gain = dt("attn_q_gain", [2, 3, 64], F32, kind="ExternalInput").ap()
    attn_k_gain = dt("attn_k_gain", [2, 3, 64], F32, kind="ExternalInput").ap()
    attn_w_out = dt("attn_w_out", [2, 1024, 1024], F32, kind="ExternalInput").ap()
    rel_bias = dt("rel_bias", [32, 48], F32, kind="ExternalInput").ap()
    y_out = dt("y", [T, D_MODEL], F32, kind="ExternalOutput").ap()

    has_attn = any(l % 2 == 1 for l in layers)
    has_conv = any(l % 2 == 0 for l in layers)
    FD = dt("fd", [3, 16, FPER], F32)
    FT = dt("ft", [3, 16, 128 * FPER], F32)
    SNDX = dt("sndx", [2, D_MODEL], F32)
    RCVX = dt("rcvx", [4, D_MODEL], F32)
    SND = {}
    RCV = {}
    for l in layers:
        if l % 2 == 1:
            for hp in range(8):
                for g, dil in enumerate(GROUP_DIL):
                    SND[(l, hp, g)] = (dt("sndk_%d_%d_%d" % (l, hp, g), [128, 128 * dil], BF16), dt("sndv_%d_%d_%d" % (l, hp, g), [128, 192 * dil], BF16))
                    RCV[(l, hp, g)] = (dt("rcvk_%d_%d_%d" % (l, hp, g), [256, 128 * dil], BF16), dt("rcvv_%d_%d_%d" % (l, hp, g), [256, 192 * dil], BF16))

    sbt = nc.alloc_sbuf_tensor
    X = sbt("X", [128, NT, D_MODEL], F32)
    HT = sbt("HT", [128, 8, T], BF16)
    GB = sbt("GB", [128, D_MODEL], F32)
    HN = [sbt("HN%d" % i, [128, D_MODEL], BF16) for i in range(2)]
    IDENT = sbt("IDENT", [128, 128], BF16)
    ONESBD = sbt("ONESBD", [128, 128], BF16)
    SS = sbt("SS", [128, NT + 1], F32)
    RSTD = sbt("RSTD", [128, NT + 1], F32)
    FLAG = sbt("FLAG", [128, 1], F32)
    HTH = sbt("HTH", [128, 8, 2], BF16)
    EPSC = sbt("EPSC", [128, 1], F32)
    JOINT = sbt("JOINT", [128, 2], F32)
    NEGF = sbt("NEGF", [128, 1], F32)
    c_CW = sbt("cCW", [128, 2, 16, 3], F32)
    a_GQK = sbt("aGQK", [128, 2, 6], F32)
    abase = (nc._sbuf_addr_for_side(None) + 63) // 64 * 64
    alimit = nc._sbuf_addr_for_side(None) + nc.sbuf_bytes_remaining
    esz = {F32: 4, BF16: 2}
    cur = [abase]

    def arena(name, shape, dtype):
        n = esz[dtype]
        for d in shape[1:]:
            n *= d
        n = (n + 63) // 64 * 64
        t = nc.alloc_sbuf_tensor_at(name, shape, dtype, offset=cur[0])
        cur[0] += n
        assert cur[0] <= alimit, ("SBUF arena overflow", name, cur[0], alimit)
        return t

    if has_conv:
        cur[0] = abase
        c_WJ = [arena("cWJ%d" % i, [128, 8, 4, 128], BF16) for i in range(2)]
        c_WO = [arena("cWO%d" % i, [128, 4, D_MODEL], BF16) for i in range(2)]
        c_YT = arena("cYT", [128, 4, T], BF16)
        c_VV = [arena("cVV%d" % i, [128, 514], F32) for i in range(2)]
        c_CS = [arena("cCS%d" % i, [128, 512], F32) for i in range(2)]
        c_SZ = [arena("cSZ%d" % i, [128, 512], F32) for i in range(2)]
        c_T1 = [arena("cT1%d" % i, [128, 512], F32) for i in range(2)]
        c_CH = arena("cCH", [128, 2], F32)
        XH = arena("XH", [2, D_MODEL], F32)
        HNH = arena("HNH", [2, D_MODEL], BF16)
        conv_end = cur[0]
    if has_attn:
        cur[0] = abase
        a_WA = [arena("aWA%d" % i, [128, 8, 3, 128], BF16) for i in range(2)]
        a_WZ = [arena("aWZ0", [128, 8, 128], BF16)] * 2
        a_WO = arena("aWO", [128, 2, D_MODEL], BF16)
        a_YT = arena("aYT", [128, 2, T], BF16)
        a_ACC = arena("aACC", [128, 2, T], F32)
        a_SZ = arena("aSZ", [128, T], BF16)
        a_QT = arena("aQT", [128, T], BF16)
        a_KT = arena("aKT", [128, T], BF16)
        a_VB = arena("aVB", [128, 16, 192], BF16)
        a_KTH = [arena("aKTH%d" % g, [128, 128 * d], BF16) for g, d in enumerate(GROUP_DIL)]
        a_VBH = [arena("aVBH%d" % g, [128, d, 192], BF16) for g, d in enumerate(GROUP_DIL)]
        a_BT = [arena("aBT%d" % i, [128, 2, 256], F32) for i in range(2)]
        a_SQ = [arena("aSQ%d" % i, [128, 512], BF16) for i in range(2)]
        a_RS = [arena("aRS%d" % i, [128, 512], F32) for i in range(2)]
        a_E = [arena("aE%d" % i, [128, 256], F32) for i in range(4)]
        a_PT = [arena("aPT%d" % i, [128, 256], BF16) for i in range(8)]
        a_PTH = [arena("aPTH%d" % i, [128, 128], BF16) for i in range(6)]
        a_R = [arena("aR%d" % i, [128, 256], F32) for i in range(2)]
        a_TMP = [arena("aTMP%d" % i, [128, 256], F32) for i in range(2)]
        boff = abase + 81920
        assert (not has_conv) or conv_end <= boff
        a_RB = nc.alloc_sbuf_tensor_at("aRB", [64, 48], F32, offset=boff)
        a_OH = nc.alloc_sbuf_tensor_at("aOH", [64, 3, FPER], F32, offset=boff + 256)
        a_F = nc.alloc_sbuf_tensor_at("aF", [16, 3, FPER], F32, offset=boff + 256 + 4608)
        assert boff + 256 + 2 * 4608 <= alimit
    PSB = [nc.alloc_psum_tensor("ps%d" % i, [128, 512], F32) for i in range(7)]
    PST = nc.alloc_psum_tensor("pst", [128, 8, 128], BF16)
    PST_F32 = PST[:].rearrange("p a b -> p (a b)").bitcast(F32)

    stack = ExitStack()
    with stack:
        P = Prog(nc, stack)
        block = stack.enter_context(nc.Block())

        bX = [Buf("X%d" % i) for i in range(NT)]
        bHT = [Buf("HT%d" % i) for i in range(NT)]
        bGB = Buf(); bHN = [Buf(), Buf()]; bCONST = Buf(); bSS = Buf(); bRSTD = Buf()
        bXH = Buf(); bHNH = Buf(); bHTH = Buf(); bPST = Buf()
        bPS = [Buf("ps%d" % i) for i in range(7)]
        ps_next = [0]
        ps_pool = [list(range(7))]
        acc_next = [0]

        def psum():
            pool = ps_pool[0]
            i = pool[ps_next[0] % len(pool)]
            ps_next[0] += 1
            if i == 7:
                return PST_F32, bPST
            return PSB[i], bPS[i]

        def psum_acc():
            i = 5 + (acc_next[0] % 2)
            acc_next[0] += 1
            return PSB[i], bPS[i]

        g0row = (conv_norm if layers[0] % 2 == 0 else attn_norm)[layers[0] // 2]
        P.dma("sp", GB[:], bass.AP(g0row.tensor, g0row.offset, [[0, 128], [1, D_MODEL]]), wr=[bGB])
        gb_preloaded = [True]
        xv = x_in.rearrange("(i p) d -> p i d", p=128)
        for i in range(NT):
            P.dma("sp", X[:, i, :], xv[:, i, :], wr=[bX[i]])
        if layers[0] % 2 == 0:
            P.dma("sp", XH[:], xh_in, wr=[bXH])
        ctk = []
        ctk.append(P.dma("pool", IDENT[:], ident_in))
        ctk.append(P.dma("pool", ONESBD[:], onesbd_in))
        bFLAG = Buf()
        P.dma("sp", FLAG[:], flag_in, wr=[bFLAG])
        if FORCE_NEGF:
            P.memset("pool", NEGF[:], -30000.0, wr=[bFLAG])
        else:
            P.op("dve", lambda e: e.tensor_scalar(out=NEGF[:], in0=FLAG[:], scalar1=-1.0, scalar2=30000.0, op0=ALU.add, op1=ALU.mult),
                 rd=[bFLAG], wr=[bFLAG])
        P.op("pool", lambda e: e.memset(EPSC[:], EPS), extra=ctk, wr=[bCONST])
        cwk, gqk = [], []
        for jl in range(2):
            if (2 * jl) in layers:
                for kt in range(3):
                    cwk.append(P.dma("sp", c_CW[:, jl, :, kt], conv_w[jl, kt].rearrange("(j p) -> p j", p=128),
                                     allow_slow_non_contiguous=True))
            if (2 * jl + 1) in layers:
                for half in range(2):
                    gqk.append(P.dma("sp", a_GQK[half * 64:(half + 1) * 64, jl, 0:3], attn_q_gain[jl].rearrange("g d -> d g"),
                                     allow_slow_non_contiguous=True))
                    gqk.append(P.dma("sp", a_GQK[half * 64:(half + 1) * 64, jl, 3:6], attn_k_gain[jl].rearrange("g d -> d g"),
                                     allow_slow_non_contiguous=True))
        bCW = Buf(); bGQK = Buf()
        P.op("pool", lambda e: e.memset(JOINT[:, 0:1], 0.0), extra=cwk, wr=[bCW])
        P.op("pool", lambda e: e.memset(JOINT[:, 1:2], 0.0), extra=gqk, wr=[bGQK])

        def norm_rows(xap, np_, hn, bx, bhn, col):
            ssc = SS[0:np_, col:col + 1]
            rsc = RSTD[0:np_, col:col + 1]
            P.act(hn[0:np_, :], xap, AF.Square, rd=[bx], wr=[bhn, bSS], accum_out=ssc)
            P.act(rsc, ssc, AF.Ln, rd=[bSS, bCONST], wr=[bRSTD], scale=1.0 / D_MODEL, bias=EPSC[0:np_, :])
            P.act(rsc, rsc, AF.Exp, rd=[bRSTD], wr=[bRSTD], scale=-0.5)
            P.stt("dve", hn[0:np_, :], xap, rsc, GB[0:np_, :], ALU.mult, ALU.mult, rd=[bx, bRSTD, bGB], wr=[bhn])

        def norm_phase(gain_dram_row, with_halo):
            gsrc = bass.AP(gain_dram_row.tensor, gain_dram_row.offset, [[0, 128], [1, D_MODEL]])
            if gb_preloaded[0]:
                gb_preloaded[0] = False
            else:
                P.dma("sp", GB[:], gsrc, wr=[bGB])
            P.memset("pool", SS[:], 0.0, wr=[bSS])
            for i in range(NT):
                hn = HN[i % 2]
                norm_rows(X[:, i, :], 128, hn, bX[i], bHN[i % 2], i)
                for c in range(8):
                    P.tr(PST[:, c, :], hn[:, c * 128:(c + 1) * 128], IDENT[:], rd=[bHN[i % 2], bCONST], wr=[bPST], sig=(c == 7))
                P.cp("dve", HT[:, :, i * 128:(i + 1) * 128], PST[:], rd=[bPST], wr=[bHT[i]])
            while deferred:
                deferred.pop(0)()
            if with_halo:
                norm_rows(XH[:], 2, HNH, bXH, bHNH, NT)
                for c in range(8):
                    P.tr(PST[:, c, 0:2], HNH[0:2, c * 128:(c + 1) * 128], IDENT[0:2, 0:2], rd=[bHNH, bCONST], wr=[bPST], sig=(c == 7))
                P.acopy(HTH[:], PST[:, :, 0:2], rd=[bPST], wr=[bHTH])

        def out_proj(YT, bYT, WO, bWO, nk, first_last):
            order = list(range(NT))
            if first_last:
                order = [NT - 1] + list(range(NT - 1))
            for i in order:
                for nh in range(2):
                    po, bpo = psum()
                    for kk in range(nk):
                        P.mm(po[:, :], YT[:, kk, i * 128:(i + 1) * 128], WO[:, kk, nh * 512:(nh + 1) * 512],
                             kk == 0, kk == nk - 1, rd=[bWO] + bYT, wr=[bpo], sig=(kk == nk - 1))
                    xs = X[:, i, nh * 512:(nh + 1) * 512]
                    P.tt("dve", xs, po[:, :], xs, ALU.add, rd=[bpo], wr=[bX[i]])

        def conv_layer(jl):
            winv = conv_w_in[jl].rearrange("(k p) (s j c) -> p k s j c", p=128, s=4, j=16, c=128)
            woutv = conv_w_out[jl].rearrange("(r kk p) n -> p r kk n", p=128, kk=4)
            bWJ = [Buf(), Buf()]; bWO = [Buf(), Buf()]; bYT = [Buf() for _ in range(4)]
            bVV = [Buf(), Buf()]; bCS = [Buf(), Buf()]; bSZ = [Buf(), Buf()]; bT1 = [Buf(), Buf()]
            bCH = Buf()

            def load_wj(j):
                for s in range(4):
                    P.dma("pool", c_WJ[j % 2][:, :, s, :], winv[:, :, s, j, :], wr=[bWJ[j % 2]])

            def load_wo(r):
                for kk in range(4):
                    P.dma("pool", c_WO[r % 2][:, kk, :], woutv[:, r, kk, :], wr=[bWO[r % 2]])

            load_wj(0)
            norm_phase(conv_norm[jl], True)
            it = 0
            for j in range(16):
                if j + 1 < 16:
                    load_wj(j + 1)
                if j % 4 == 0:
                    load_wo(j // 4)
                WJ = c_WJ[j % 2]
                bw = bWJ[j % 2]
                ph, bph = psum()
                for sec in (1, 2):
                    for k in range(8):
                        P.mm(ph[:, (sec - 1) * 2:(sec - 1) * 2 + 2], WJ[:, k, sec, :], HTH[:, k, :], k == 0, k == 7,
                             rd=[bw, bHTH], wr=[bph], sig=(sec == 2 and k == 7))
                P.acopy(c_CH[:], ph[:, 0:2], rd=[bph], wr=[bCH])
                for s in range(4):
                    vb = it % 2
                    it += 1
                    VV = c_VV[vb]; CS = c_CS[vb]; SZ = c_SZ[vb]; T1 = c_T1[vb]
                    pss = {}
                    for sec in (1, 2, 3, 0):
                        pq, bpq = psum()
                        pss[sec] = (pq, bpq)
                        for k in range(8):
                            P.mm(pq[:, :], WJ[:, k, sec, :], HT[:, k, s * 512:(s + 1) * 512], k == 0, k == 7,
                                 rd=[bw] + bHT[4 * s:4 * s + 4], wr=[bpq], sig=(k == 7))
                    pc, bpc = pss[1]; pu, bpu = pss[2]; pz, bpz = pss[3]; pb, bpb = pss[0]
                    P.acopy(CS[:], pc[:, :], rd=[bpc], wr=[bCS[vb]])
                    if s == 0:
                        P.tt("dve", VV[:, 0:2], c_CH[:], ph[:, 2:4], ALU.mult, rd=[bCH, bph], wr=[bVV[vb]])
                    else:
                        P.cp("dve", VV[:, 0:2], c_VV[1 - vb][:, 512:514], rd=[bVV[1 - vb]], wr=[bVV[vb]])
                    P.tt("dve", VV[:, 2:514], CS[:], pu[:, :], ALU.mult, rd=[bCS[vb], bpu], wr=[bVV[vb]])
                    P.act(SZ[:], pz[:, :], AF.Silu, rd=[bpz], wr=[bSZ[vb]])
                    P.ts("dve", T1[:], VV[:, 2:514], c_CW[:, jl, j, 2:3], ALU.mult, rd=[bVV[vb], bCW], wr=[bT1[vb]])
                    P.stt("dve", T1[:], VV[:, 1:513], c_CW[:, jl, j, 1:2], T1[:], ALU.mult, ALU.add, rd=[bVV[vb], bT1[vb]], wr=[bT1[vb]])
                    P.stt("dve", T1[:], VV[:, 0:512], c_CW[:, jl, j, 0:1], T1[:], ALU.mult, ALU.add, rd=[bVV[vb], bT1[vb]], wr=[bT1[vb]])
                    P.tt("dve", T1[:], T1[:], pb[:, :], ALU.mult, rd=[bT1[vb], bpb], wr=[bT1[vb]])
                    P.tt("pool", c_YT[:, j % 4, s * 512:(s + 1) * 512], T1[:], SZ[:], ALU.mult, rd=[bT1[vb], bSZ[vb]], wr=[bYT[j % 4]])
                if j % 4 == 3:
                    r = j // 4
                    out_proj(c_YT, bYT, c_WO[r % 2], bWO[r % 2], 4, r == 3)

        def bias_setup():
            bRB = Buf(); bOH = Buf(); bF = Buf(); bFD = Buf(); bFT = Buf()
            P.memset("pool", a_RB[:], 0.0, wr=[bRB])
            P.memset("pool", a_RB[32:33, :], 1.0, wr=[bRB])
            P.dma("sp", a_RB[0:32, :], rel_bias, wr=[bRB])
            P.dma("sp", a_OH[:], oh_in, wr=[bOH])
            for g in range(3):
                ps, bps = psum()
                P.mm(ps[0:16, 0:FPER], a_RB[:, g * 16:(g + 1) * 16], a_OH[:, g, :], True, True, rd=[bRB, bOH], wr=[bps])
                P.cp("dve", a_F[:, g, :], ps[0:16, 0:FPER], rd=[bps], wr=[bF])
            P.dma("sp", FD.ap().rearrange("g h j -> h g j"), a_F[:], rd=[bF], wr=[bFD])
            for g in range(3):
                src = bass.AP(FD, g * 16 * FPER, [[FPER, 16], [0, 128], [1, FPER]])
                dst = bass.AP(FT, g * 16 * 128 * FPER, [[128 * FPER, 16], [FPER, 128], [1, FPER]])
                P.dma("sp", dst, src, rd=[bFD], wr=[bFT])
            return bFT

        def attn_layer(jl, l, bFT):
            win = attn_w_in[jl]
            wqkv = win[:, 0:9216].rearrange("(k p) (g w h c) -> p k g w h c", p=128, g=3, w=3, h=8, c=128)
            wz = win[:, 9216:10240].rearrange("(k p) (h c) -> p k h c", p=128, c=128)
            woutv = attn_w_out[jl].rearrange("(r kk p) n -> p r kk n", p=128, kk=2)
            bWA = [Buf(), Buf()]; bWZ = [Buf()] * 2; bWO = Buf(); bYT = [Buf(), Buf()]
            bACC = Buf(); bSZ = Buf(); bQT = Buf(); bKT = Buf(); bVB = Buf()
            bKTH = [Buf() for _ in range(3)]; bVBH = [Buf() for _ in range(3)]
            bBT = [Buf(), Buf()]; bSQ = [Buf(), Buf()]; bRS = [Buf(), Buf()]; bE = [Buf() for _ in range(4)]
            bPT = [Buf() for _ in range(8)]
            bPTH = [Buf() for _ in range(6)]
            bR = [Buf(), Buf()]; bTMP = [Buf(), Buf()]

            def load_wa(hp, g, buf):
                for w in range(3):
                    P.dma("pool", a_WA[buf][:, :, w, :], wqkv[:, :, g, w, hp, :], wr=[bWA[buf]])

            def load_bt(hp, g, buf):
                src = bass.AP(FT, ((g * 16 + 2 * hp) * 128 * FPER) + 127, [[FPER - 1, 128], [128 * FPER, 2], [1, 256]])
                P.dma("sp", a_BT[buf][:], src, rd=[bFT], wr=[bBT[buf]])

            load_wa(0, 0, 0)
            norm_phase(attn_norm[jl], False)
            P.memset("pool", a_VB[:, :, 64:128], 1.0, wr=[bVB])
            e_next = [0]
            pending_out = [False]
            pt_next = [0]
            pth_next = [0]
            it = 0
            for hp in range(8):
                P.dma("pool", a_WZ[hp % 2][:], wz[:, :, hp, :], wr=[bWZ[hp % 2]])
                def z_chunk(s):
                    ps, bps = psum()
                    for k in range(8):
                        P.mm(ps[:, :], a_WZ[hp % 2][:, k, :], HT[:, k, s * 512:(s + 1) * 512], k == 0, k == 7,
                             rd=[bWZ[hp % 2]] + bHT[4 * s:4 * s + 4], wr=[bps], sig=(k == 7))
                    P.act(a_SZ[:, s * 512:(s + 1) * 512], ps[:, :], AF.Silu, rd=[bps], wr=[bSZ])
                z_plan = {0: (0, 1), 1: (2,), 2: (3,)}
                for g, dil in enumerate(GROUP_DIL):
                    wbuf = it % 2
                    it += 1
                    nb = 16 // dil
                    WA = a_WA[wbuf]
                    if not (hp == 7 and g == 2):
                        nhp, ng = (hp, g + 1) if g < 2 else (hp + 1, 0)
                        load_wa(nhp, ng, 1 - wbuf)
                    if hp % 2 == 0 and g == 2:
                        for kk in range(2):
                            P.dma("pool", a_WO[:, kk, :], woutv[:, hp // 2, kk, :], wr=[bWO])
                    load_bt(hp, g, wbuf)
                    BT = a_BT[wbuf]

                    def qk_proj(w, dest, bdest, gcol):
                        dvr = dest[:].rearrange("p (r s) -> p r s", r=dil)
                        n_s = 512 // dil
                        st = {}

                        def stage_p(s):
                            ps, bps = psum()
                            for k in range(8):
                                P.mm(ps[:, :], WA[:, k, w, :], HT[:, k, s * 512:(s + 1) * 512], k == 0, k == 7,
                                     rd=[bWA[wbuf]] + bHT[4 * s:4 * s + 4], wr=[bps], sig=(k == 7))
                            sb = s % 2
                            P.act(a_SQ[sb][:], ps[:, :], AF.Square, rd=[bps], wr=[bSQ[sb]])
                            st[s] = (ps, bps)

                        def stage_o(s):
                            ps, bps = st.pop(s)
                            sb = s % 2
                            ps2, bps2 = psum()
                            P.mm(ps2[:, :], ONESBD[:], a_SQ[sb][:], True, True, rd=[bSQ[sb], bCONST], wr=[bps2])
                            P.act(a_RS[sb][:], ps2[:, :], AF.Ln, rd=[bps2, bCONST], wr=[bRS[sb]], scale=1.0 / 64, bias=EPSC[:, :])
                            P.act(a_RS[sb][:], a_RS[sb][:], AF.Exp, rd=[bRS[sb]], wr=[bRS[sb]], scale=-0.5)
                            P.stt("dve", dvr[:, :, s * n_s:(s + 1) * n_s], ps[:, :].rearrange("p (s r) -> p r s", r=dil),
                                  a_GQK[:, jl, gcol:gcol + 1], a_RS[sb][:].rearrange("p (s r) -> p r s", r=dil),
                                  ALU.mult, ALU.mult, rd=[bps, bRS[sb], bGQK], wr=[bdest])

                        stage_p(0)
                        for s in range(1, 4):
                            stage_p(s)
                            stage_o(s - 1)
                        return lambda: stage_o(3)

                    k_tail = qk_proj(1, a_KT, bKT, 3 + g)
                    for qd in range(4):
                        if qd == 1:
                            k_tail()
                        ps, bps = psum()
                        for bb in range(4):
                            blk = qd * 4 + bb
                            r, n = blk // nb, blk % nb
                            t0 = r + dil * 128 * n
                            for k in range(8):
                                P.mm(ps[:, bb * 128:(bb + 1) * 128], HT[:, k, t0:t0 + dil * 127 + 1:dil], WA[:, k, 2, :], k == 0, k == 7,
                                     rd=[bWA[wbuf]] + bHT, wr=[bps], sig=(bb == 3 and k == 7))
                        dst = sb_ap(a_VB, qd * 4 * 192, [[192, 4], [128, 2], [1, 64]])
                        P.acopy(dst, ps[:, :].rearrange("p (b h d) -> p b h d", b=4, h=2), rd=[bps], wr=[bVB])
                    if pending_out[0]:
                        pending_out[0] = False
                        out_proj(a_YT, bYT, a_WO, bWO, 2, False)
                    sndk, sndv = SND[(l, hp, g)]; rcvk, rcvv = RCV[(l, hp, g)]
                    bSk = Buf(); bSv = Buf(); bRk = Buf(); bRv = Buf()
                    ksrc = sb_ap(a_KT, (nb - 1) * 128, [[nb * 128, dil], [1, 128]])
                    P.dma("sp", sndk.ap().rearrange("p (r c) -> p r c", c=128), ksrc, rd=[bKT], wr=[bSk])
                    vsrc = sb_ap(a_VB, (nb - 1) * 192, [[nb * 192, dil], [1, 192]])
                    P.dma("sp", sndv.ap().rearrange("p (r c) -> p r c", c=192), vsrc, rd=[bVB], wr=[bSv])
                    P.collective(lambda e, a=sndk, b_=rcvk: e.collective_compute(
                        "AllGather", ALU.bypass, replica_groups=PAIRS, ins=[a.ap().opt()], outs=[b_.ap().opt()]),
                        rd=[bSk], wr=[bRk])
                    P.collective(lambda e, a=sndv, b_=rcvv: e.collective_compute(
                        "AllGather", ALU.bypass, replica_groups=PAIRS, ins=[a.ap().opt()], outs=[b_.ap().opt()]),
                        rd=[bSv], wr=[bRv])
                    P.dma("sp", a_KTH[g][:], rcvk.ap()[0:128, :], rd=[bRk], wr=[bKTH[g]])
                    P.dma("sp", a_VBH[g][:], rcvv.ap()[0:128, :].rearrange("p (r c) -> p r c", c=192), rd=[bRv], wr=[bVBH[g]])
                    q_tail = qk_proj(0, a_QT, bQT, g)
                    zc = z_plan[g]
                    z_chunk(zc[0])
                    q_tail()
                    for s_ in zc[1:]:
                        z_chunk(s_)
                    LA = 5
                    assert len(a_PT) >= LA + 3 and len(a_PTH) >= LA + 1

                    def score(head, kt_ap, bkt, q0, N, btc0, ptile, bpt, halo=False):
                        rows = slice(64 * head, 64 * head + 64)
                        ps, bps = psum()
                        P.mm(ps[:, 0:N], kt_ap, a_QT[rows, q0:q0 + N], True, True, rd=[bkt, bQT], wr=[bps])
                        ei = e_next[0]
                        e_next[0] = (ei + 1) % len(a_E)
                        P.stt("dve", a_E[ei][:, 0:N], ps[:, 0:N], 0.125, BT[:, head, btc0:btc0 + N], ALU.mult, ALU.add,
                              rd=[bps, bBT[wbuf]], wr=[bE[ei]])
                        if halo:
                            P.act(ptile[:, 0:N], a_E[ei][:, 0:N], AF.Exp, rd=[bE[ei], bFLAG], wr=[bpt], bias=NEGF[:, :])
                        else:
                            P.act(ptile[:, 0:N], a_E[ei][:, 0:N], AF.Exp, rd=[bE[ei]], wr=[bpt])

                    tasks = [(head, b) for b in range(16) for head in range(2)]
                    slot = {}
                    hslot = {}
                    acc_banks = {}

                    def emit_scores(head, b):
                        rows = slice(64 * head, 64 * head + 64)
                        r, n = b // nb, b % nb
                        N = 256 if n < nb - 1 else 128
                        sl = pt_next[0] % len(a_PT)
                        pt_next[0] += 1
                        slot[(head, b)] = sl
                        score(head, a_KT[rows, b * 128:(b + 1) * 128], bKT, b * 128, N, 0, a_PT[sl], bPT[sl])
                        if n == 0 and not NOHALO:
                            hs = pth_next[0] % len(a_PTH)
                            pth_next[0] += 1
                            hslot[(head, b)] = hs
                            score(head, a_KTH[g][rows, r * 128:(r + 1) * 128], bKTH[g], b * 128, 128, 128, a_PTH[hs], bPTH[hs], halo=True)

                    def emit_pv(head, b):
                        vc = slice(64 * head, 64 * head + 128)
                        r, n = b // nb, b % nb
                        qd, bb = b // 4, b % 4
                        if bb == 0:
                            acc_banks[(head, qd)] = psum_acc()
                        po, bpo = acc_banks[(head, qd)]
                        pob = po[:, bb * 128:(bb + 1) * 128]
                        sl = slot[(head, b)]
                        if n == 0 and NOHALO:
                            pass
                        elif n == 0:
                            hs = hslot[(head, b)]
                            P.mm(pob, a_VBH[g][:, r, vc], a_PTH[hs][:, 0:128], True, False, rd=[bVBH[g], bPTH[hs]], wr=[bpo], sig=False)
                        else:
                            sp_ = slot[(head, b - 1)]
                            P.mm(pob, a_VB[:, b - 1, vc], a_PT[sp_][:, 128:256], True, False, rd=[bVB, bPT[sp_]], wr=[bpo], sig=False)
                        P.mm(pob, a_VB[:, b, vc], a_PT[sl][:, 0:128], (n == 0 and NOHALO), True, rd=[bVB, bPT[sl]], wr=[bpo], sig=(bb == 3))
                        if bb == 3:
                            b0 = qd * 4
                            if dil == 1:
                                acc = sb_ap(a_ACC, head * T + b0 * 128, [[1, 512]])
                                pin = po[:, :]
                            elif dil == 4:
                                acc = sb_ap(a_ACC, head * T + qd, [[4, 512]])
                                pin = po[:, :]
                            else:
                                acc = sb_ap(a_ACC, head * T + b0, [[1, 4], [16, 128]])
                                pin = po[:, :].rearrange("p (b i) -> p b i", b=4)
                            if g == 0:
                                P.cp("dve", acc, pin, rd=[bpo], wr=[bACC])
                            else:
                                P.tt("dve", acc, pin, acc, ALU.add, rd=[bpo], wr=[bACC])

                    for i in range(len(tasks) + LA):
                        if i < len(tasks):
                            emit_scores(*tasks[i])
                        if i >= LA:
                            emit_pv(*tasks[i - LA])
                for c in range(8):
                    cb = c % 2
                    cs = slice(c * 256, (c + 1) * 256)
                    P.act(a_R[cb][0:64, :], a_ACC[64:128, 0, cs], AF.Ln, rd=[bACC], wr=[bR[cb]])
                    P.act(a_R[cb][64:128, :], a_ACC[0:64, 1, cs], AF.Ln, rd=[bACC], wr=[bR[cb]])
                    P.act(a_R[cb][:, :], a_R[cb][:, :], AF.Exp, rd=[bR[cb]], wr=[bR[cb]], scale=-1.0)
                    P.tt("dve", a_TMP[cb][0:64, :], a_ACC[0:64, 0, cs], a_R[cb][0:64, :], ALU.mult, rd=[bACC, bR[cb]], wr=[bTMP[cb]])
                    P.tt("dve", a_TMP[cb][64:128, :], a_ACC[64:128, 1, cs], a_R[cb][64:128, :], ALU.mult, rd=[bACC, bR[cb]], wr=[bTMP[cb]])
                    P.tt("pool", a_YT[:, hp % 2, cs], a_TMP[cb][:], a_SZ[:, cs], ALU.mult, rd=[bTMP[cb], bSZ], wr=[bYT[hp % 2]])
                if hp % 2 == 1:
                    if hp == 7:
                        out_proj(a_YT, bYT, a_WO, bWO, 2, True)
                    else:
                        pending_out[0] = True

        bFT_box = [None]
        deferred = []
        if has_attn:
            if layers[0] % 2 == 1:
                bFT_box[0] = bias_setup()
                P.barrier()
            else:
                deferred.append(lambda: bFT_box.__setitem__(0, bias_setup()))
        for li, l in enumerate(layers):
            if li > 0:
                P.barrier()
            if l % 2 == 0:
                if li > 0:
                    bSX = Buf(); bRX = Buf()
                    P.dma("sp", SNDX.ap(), X[126:128, NT - 1, :], rd=[bX[NT - 1]], wr=[bSX])
                    P.collective(lambda e: e.collective_compute("AllGather", ALU.bypass, replica_groups=PAIRS,
                                                                ins=[SNDX.ap().opt()], outs=[RCVX.ap().opt()]), rd=[bSX], wr=[bRX])
                    P.dma("sp", XH[:], RCVX.ap()[0:2, :], rd=[bRX], wr=[bXH])
                    P.ts("dve", XH[:], XH[:], FLAG[0:2, 0:1], ALU.mult, rd=[bFLAG], wr=[bXH])
                ps_pool[0] = list(range(7))
                conv_layer(l // 2)
            else:
                ps_pool[0] = [0, 1, 2, 3, 4, 7]
                attn_layer(l // 2, l, bFT_box[0])

        yv = y_out.rearrange("(i p) d -> p i d", p=128)
        tks = []
        for i in range(NT):
            tks.append(P.dma("sp", yv[:, i, :], X[:, i, :], rd=[bX[i]]))
        P.wait_all("sp", tks)
        P.emit(block)
    return nc


_CONST = {}


def _consts():
    if not _CONST:
        bd = np.zeros((128, 128), np.float32)
        bd[0:64, 0:64] = 1.0
        bd[64:128, 64:128] = 1.0
        _CONST["ident"] = np.eye(128, dtype=np.float32)
        _CONST["onesbd"] = bd
        _CONST["oh"] = bias_onehot()
    return _CONST


_PROGS = {}


def _run(layers, x_shards, xh_shards, weights):
    key = tuple(layers)
    if key not in _PROGS:
        _PROGS[key] = build(list(layers))
    nc = _PROGS[key]
    c = _consts()
    in_maps = []
    for core in range(NCORES):
        m = dict(weights)
        m["x"] = x_shards[core]
        m["xh"] = xh_shards[core]
        m["flag"] = np.full((128, 1), float(core % 2), np.float32)
        m["ident"] = c["ident"]
        m["onesbd"] = c["onesbd"]
        m["oh"] = c["oh"]
        in_maps.append(m)
    res = run_bass_kernel_spmd(nc, in_maps, core_ids=list(range(NCORES)))
    return [r["y"] for r in res.results]


def _shard(x):
    xs, xh = [], []
    for core in range(NCORES):
        b, half = core // 2, core % 2
        xs.append(np.ascontiguousarray(x[b, half * T:(half + 1) * T, :]))
        if half == 0:
            xh.append(np.zeros((2, D_MODEL), np.float32))
        else:
            xh.append(np.ascontiguousarray(x[b, T - 2:T, :]))
    return xs, xh


def _unshard(ys):
    out = np.empty((4, 2 * T, D_MODEL), np.float32)
    for core in range(NCORES):
        b, half = core // 2, core % 2
        out[b, half * T:(half + 1) * T, :] = ys[core]
    return out


def kernel(x, conv_norm, conv_w_in, conv_w, conv_w_out, attn_norm, attn_w_in,
           attn_q_gain, attn_k_gain, attn_w_out, rel_bias, _layers=None):
    f = lambda a: np.ascontiguousarray(np.asarray(a, dtype=np.float32))
    weights = dict(conv_norm=f(conv_norm), conv_w_in=f(conv_w_in), conv_w=f(conv_w), conv_w_out=f(conv_w_out),
                   attn_norm=f(attn_norm), attn_w_in=f(attn_w_in), attn_q_gain=f(attn_q_gain),
                   attn_k_gain=f(attn_k_gain), attn_w_out=f(attn_w_out), rel_bias=f(rel_bias))
    x = f(x)
    if _layers is not None:
        groups = _layers
    elif FUSED:
        groups = [[0, 1, 2, 3]]
    else:
        groups = [[0], [1], [2], [3]]
    for grp in groups:
        xs, xh = _shard(x)
        ys = _run(grp, xs, xh, weights)
        x = _unshard(ys)
    return x
```

```python
import math
import numpy as np
import concourse.bass as bass
import concourse.mybir as mybir
from concourse.bass_utils import run_bass_kernel_spmd

F32 = mybir.dt.float32
BF16 = mybir.dt.bfloat16
AF = mybir.ActivationFunctionType
ALU = mybir.AluOpType

NCORES = 8
T = 2048
NT = T // 128
D_MODEL = 1024
EPS = 1e-6
GROUP_DIL = (1, 4, 16)
FUSED = True
import os
NOHALO = bool(int(os.environ.get('NOHALO', '0')))
FORCE_NEGF = bool(int(os.environ.get('FORCE_NEGF', '0')))
PAIRS = [[0, 1], [2, 3], [4, 5], [6, 7]]
FPER = 382


class Buf:
    __slots__ = ("w", "r", "name")

    def __init__(self, name=""):
        self.w = None
        self.r = {}
        self.name = name


class Prog:
    ENG = ("pe", "act", "dve", "pool", "sp")

    def __init__(self, nc, stack):
        self.nc = nc
        self.ops = {e: [] for e in self.ENG}
        self.sem = {}
        self.cnt = {}
        self.waited = {e: {} for e in self.ENG}
        for e in self.ENG:
            self.sem[e] = stack.enter_context(nc.semaphore("s_" + e))
            self.cnt[e] = 0
        self.ndma = 24
        self.dsem = []
        for q in ("sp", "pool"):
            for i in range(self.ndma):
                nm = "d_%s_%d" % (q, i)
                self.sem[nm] = stack.enter_context(nc.semaphore(nm))
                self.cnt[nm] = 0
        self.dnext = {"sp": 0, "pool": 0}
        self.sem["cc"] = stack.enter_context(nc.semaphore("s_cc"))
        self.cnt["cc"] = 0

    def _waits(self, eng, rd, wr, extra):
        need = {}

        def add(t):
            if t is None:
                return
            s, v = t
            if s == eng and v > self.cnt[eng]:
                return
            if need.get(s, 0) < v:
                need[s] = v
        for b in rd:
            add(b.w)
        for b in wr:
            add(b.w)
            for t in b.r.values():
                add(t)
        for t in extra:
            add(t)
        out = []
        wd = self.waited[eng]
        for s, v in need.items():
            if wd.get(s, 0) < v:
                wd[s] = v
                out.append((s, v))
        return out

    def _commit(self, ticket, rd, wr):
        for b in rd:
            b.r[ticket[0]] = ticket
        for b in wr:
            b.w = ticket
            b.r = {}

    def op(self, eng, fn, rd=(), wr=(), extra=(), sig=True):
        waits = self._waits(eng, rd, wr, extra)
        if sig:
            self.cnt[eng] += 1
            tk = (eng, self.cnt[eng])
            self.ops[eng].append((waits, fn, (eng, 1)))
        else:
            tk = (eng, self.cnt[eng] + 1)
            self.ops[eng].append((waits, fn, None))
        self._commit(tk, rd, wr)
        return tk

    def dma(self, q, out, in_, rd=(), wr=(), extra=(), **kw):
        i = self.dnext[q]
        self.dnext[q] = (i + 1) % self.ndma
        nm = "d_%s_%d" % (q, i)
        prev = (nm, self.cnt[nm]) if self.cnt[nm] > 0 else None
        waits = self._waits(q, rd, wr, list(extra) + [prev])
        self.cnt[nm] += 16
        tk = (nm, self.cnt[nm])
        self.ops[q].append((waits, lambda e: e.dma_start(out=out, in_=in_, **kw), (nm, 16)))
        self._commit(tk, rd, wr)
        return tk

    def collective(self, fn, rd=(), wr=()):
        waits = self._waits("pool", rd, wr, [("cc", self.cnt["cc"])] if self.cnt["cc"] else [])
        self.cnt["cc"] += 1
        tk = ("cc", self.cnt["cc"])
        self.ops["pool"].append((waits, fn, ("cc", 1)))
        self._commit(tk, rd, wr)
        return tk

    def barrier(self):
        tk = [(s, c) for s, c in self.cnt.items() if c > 0]
        for e in self.ENG:
            self.wait_all(e, tk)

    def mm(self, out, lhsT, rhs, start, stop, rd, wr, sig=True):
        return self.op("pe", lambda e: e.matmul(out, lhsT, rhs, start=start, stop=stop), rd=rd, wr=wr, sig=sig)

    def tr(self, out, in_, ident, rd, wr, sig=True):
        return self.op("pe", lambda e: e.transpose(out, in_, ident), rd=rd, wr=wr, sig=sig)

    def act(self, out, in_, func, rd, wr, **kw):
        return self.op("act", lambda e: e.activation(out=out, in_=in_, func=func, **kw), rd=rd, wr=wr)

    def acopy(self, out, in_, rd, wr):
        return self.op("act", lambda e: e.copy(out=out, in_=in_), rd=rd, wr=wr)

    def tt(self, eng, out, in0, in1, op, rd, wr):
        return self.op(eng, lambda e: e.tensor_tensor(out=out, in0=in0, in1=in1, op=op), rd=rd, wr=wr)

    def stt(self, eng, out, in0, scalar, in1, op0, op1, rd, wr):
        return self.op(eng, lambda e: e.scalar_tensor_tensor(out=out, in0=in0, scalar=scalar, in1=in1, op0=op0, op1=op1), rd=rd, wr=wr)

    def ts(self, eng, out, in0, scalar1, op0, rd, wr):
        return self.op(eng, lambda e: e.tensor_scalar(out=out, in0=in0, scalar1=scalar1, scalar2=None, op0=op0), rd=rd, wr=wr)

    def cp(self, eng, out, in_, rd, wr):
        return self.op(eng, lambda e: e.tensor_copy(out=out, in_=in_), rd=rd, wr=wr)

    def recip(self, out, in_, rd, wr):
        return self.op("dve", lambda e: e.reciprocal(out=out, in_=in_), rd=rd, wr=wr)

    def memset(self, eng, ap, val, wr):
        return self.op(eng, lambda e: e.memset(ap, val), wr=wr)

    def wait_all(self, eng, tickets):
        waits = self._waits(eng, (), (), tickets)
        self.ops[eng].append((waits, None, None))

    def emit(self, block):
        sem = self.sem

        def mk(eng):
            lst = self.ops[eng]

            def body(e):
                for waits, fn, inc in lst:
                    for s, v in waits:
                        e.wait_ge(sem[s], v)
                    if fn is None:
                        continue
                    r = fn(e)
                    if inc is not None:
                        r.then_inc(sem[inc[0]], inc[1])
            return body
        block.tensor(mk("pe"))
        block.scalar(mk("act"))
        block.vector(mk("dve"))
        block.gpsimd(mk("pool"))
        block.sync(mk("sp"))


def sb_ap(t, off, dims):
    base = t[:]
    p = base.ap[0]
    return bass.AP(t, off, [[p[0], p[1]]] + [list(d) for d in dims])


def sb_ap_p(t, p0, np_, off, dims):
    base = t[p0:p0 + np_]
    p = base.ap[0]
    return bass.AP(t, base.offset + off, [[p[0], p[1]]] + [list(d) for d in dims])


def _t5_bucket_np(dist):
    dist = np.asarray(dist, dtype=np.int64)
    d = np.maximum(dist, 1).astype(np.float32)
    large = 16 + (np.log(d / np.float32(16)) / np.float32(math.log(2048 / 16)) * np.float32(16)).astype(np.int32)
    large = np.minimum(large, 31)
    return np.where(dist < 16, dist, large).astype(np.int64)


def bias_onehot():
    oh = np.zeros((64, 3, FPER), np.float32)
    for g, dil in enumerate(GROUP_DIL):
        for j in range(FPER):
            step = j - 127
            if 0 <= step <= 128:
                b = int(_t5_bucket_np(np.array([step * dil]))[0])
                oh[b, g, j] = 1.0
            else:
                oh[32, g, j] = -30000.0
    return oh


def build(layers):
    from contextlib import ExitStack
    nc = bass.Bass("TRN2", target_bir_lowering=False)
    dt = nc.dram_tensor
    x_in = dt("x", [T, D_MODEL], F32, kind="ExternalInput").ap()
    xh_in = dt("xh", [2, D_MODEL], F32, kind="ExternalInput").ap()
    flag_in = dt("flag", [128, 1], F32, kind="ExternalInput").ap()
    ident_in = dt("ident", [128, 128], F32, kind="ExternalInput").ap()
    onesbd_in = dt("onesbd", [128, 128], F32, kind="ExternalInput").ap()
    oh_in = dt("oh", [64, 3, FPER], F32, kind="ExternalInput").ap()
    conv_norm = dt("conv_norm", [2, 1024], F32, kind="ExternalInput").ap()
    conv_w_in = dt("conv_w_in", [2, 1024, 8192], F32, kind="ExternalInput").ap()
    conv_w = dt("conv_w", [2, 3, 2048], F32, kind="ExternalInput").ap()
    conv_w_out = dt("conv_w_out", [2, 2048, 1024], F32, kind="ExternalInput").ap()
    attn_norm = dt("attn_norm", [2, 1024], F32, kind="ExternalInput").ap()
    attn_w_in = dt("attn_w_in", [2, 1024, 10240], F32, kind="ExternalInput").ap()
    attn_q_gain = dt("attn_q_gain", [2, 3, 64], F32, kind="ExternalInput").ap()
    attn_k_gain = dt("attn_k_gain", [2, 3, 64], F32, kind="ExternalInput").ap()
    attn_w_out = dt("attn_w_out", [2, 1024, 1024], F32, kind="ExternalInput").ap()
    rel_bias = dt("rel_bias", [32, 48], F32, kind="ExternalInput").ap()
    y_out = dt("y", [T, D_MODEL], F32, kind="ExternalOutput").ap()

    has_attn = any(l % 2 == 1 for l in layers)
    has_conv = any(l % 2 == 0 for l in layers)
    FD = dt("fd", [3, 16, FPER], F32)
    FT = dt("ft", [3, 16, 128 * FPER], F32)
    SNDX = dt("sndx", [2, D_MODEL], F32)
    RCVX = dt("rcvx", [4, D_MODEL], F32)
    SND = {}
    RCV = {}
    for l in layers:
        if l % 2 == 1:
            for hp in range(8):
                for g, dil in enumerate(GROUP_DIL):
                    SND[(l, hp, g)] = (dt("sndk_%d_%d_%d" % (l, hp, g), [128, 128 * dil], BF16), dt("sndv_%d_%d_%d" % (l, hp, g), [128, 192 * dil], BF16))
                    RCV[(l, hp, g)] = (dt("rcvk_%d_%d_%d" % (l, hp, g), [256, 128 * dil], BF16), dt("rcvv_%d_%d_%d" % (l, hp, g), [256, 192 * dil], BF16))

    sbt = nc.alloc_sbuf_tensor
    X = sbt("X", [128, NT, D_MODEL], F32)
    HT = sbt("HT", [128, 8, T], BF16)
    GB = sbt("GB", [128, D_MODEL], F32)
    HN = [sbt("HN%d" % i, [128, D_MODEL], BF16) for i in range(2)]
    IDENT = sbt("IDENT", [128, 128], BF16)
    ONESBD = sbt("ONESBD", [128, 128], BF16)
    SS = sbt("SS", [128, NT + 1], F32)
    RSTD = sbt("RSTD", [128, NT + 1], F32)
    FLAG = sbt("FLAG", [128, 1], F32)
    HTH = sbt("HTH", [128, 8, 2], BF16)
    EPSC = sbt("EPSC", [128, 1], F32)
    JOINT = sbt("JOINT", [128, 2], F32)
    NEGF = sbt("NEGF", [128, 1], F32)
    c_CW = sbt("cCW", [128, 2, 16, 3], F32)
    a_GQK = sbt("aGQK", [128, 2, 6], F32)
    abase = (nc._sbuf_addr_for_side(None) + 63) // 64 * 64
    alimit = nc._sbuf_addr_for_side(None) + nc.sbuf_bytes_remaining
    esz = {F32: 4, BF16: 2}
    cur = [abase]

    def arena(name, shape, dtype):
        n = esz[dtype]
        for d in shape[1:]:
            n *= d
        n = (n + 63) // 64 * 64
        t = nc.alloc_sbuf_tensor_at(name, shape, dtype, offset=cur[0])
        cur[0] += n
        assert cur[0] <= alimit, ("SBUF arena overflow", name, cur[0], alimit)
        return t

    if has_conv:
        cur[0] = abase
        c_WJ = [arena("cWJ%d" % i, [128, 8, 4, 128], BF16) for i in range(2)]
        c_WO = [arena("cWO%d" % i, [128, 4, D_MODEL], BF16) for i in range(2)]
        c_YT = arena("cYT", [128, 4, T], BF16)
        c_VV = [arena("cVV%d" % i, [128, 514], F32) for i in range(2)]
        c_CS = [arena("cCS%d" % i, [128, 512], F32) for i in range(2)]
        c_SZ = [arena("cSZ%d" % i, [128, 512], F32) for i in range(2)]
        c_T1 = [arena("cT1%d" % i, [128, 512], F32) for i in range(2)]
        c_CH = arena("cCH", [128, 2], F32)
        XH = arena("XH", [2, D_MODEL], F32)
        HNH = arena("HNH", [2, D_MODEL], BF16)
        conv_end = cur[0]
    if has_attn:
        cur[0] = abase
        a_WA = [arena("aWA%d" % i, [128, 8, 3, 128], BF16) for i in range(2)]
        a_WZ = [arena("aWZ0", [128, 8, 128], BF16)] * 2
        a_WO = arena("aWO", [128, 2, D_MODEL], BF16)
        a_YT = arena("aYT", [128, 2, T], BF16)
        a_ACC = arena("aACC", [128, 2, T], F32)
        a_SZ = arena("aSZ", [128, T], BF16)
        a_QT = arena("aQT", [128, T], BF16)
        a_KT = arena("aKT", [128, T], BF16)
        a_VB = arena("aVB", [128, 16, 192], BF16)
        a_KTH = [arena("aKTH%d" % g, [128, 128 * d], BF16) for g, d in enumerate(GROUP_DIL)]
        a_VBH = [arena("aVBH%d" % g, [128, d, 192], BF16) for g, d in enumerate(GROUP_DIL)]
        a_BT = [arena("aBT%d" % i, [128, 2, 256], F32) for i in range(2)]
        a_SQ = [arena("aSQ%d" % i, [128, 512], BF16) for i in range(2)]
        a_RS = [arena("aRS%d" % i, [128, 512], F32) for i in range(2)]
        a_E = [arena("aE%d" % i, [128, 256], F32) for i in range(4)]
        a_PT = [arena("aPT%d" % i, [128, 256], BF16) for i in range(8)]
        a_PTH = [arena("aPTH%d" % i, [128, 128], BF16) for i in range(6)]
        a_R = [arena("aR%d" % i, [128, 256], F32) for i in range(2)]
        a_TMP = [arena("aTMP%d" % i, [128, 256], F32) for i in range(2)]
        boff = abase + 81920
        assert (not has_conv) or conv_end <= boff
        a_RB = nc.alloc_sbuf_tensor_at("aRB", [64, 48], F32, offset=boff)
        a_OH = nc.alloc_sbuf_tensor_at("aOH", [64, 3, FPER], F32, offset=boff + 256)
        a_F = nc.alloc_sbuf_tensor_at("aF", [16, 3, FPER], F32, offset=boff + 256 + 4608)
        assert boff + 256 + 2 * 4608 <= alimit
    PSB = [nc.alloc_psum_tensor("ps%d" % i, [128, 512], F32) for i in range(7)]
    PST = nc.alloc_psum_tensor("pst", [128, 8, 128], BF16)
    PST_F32 = PST[:].rearrange("p a b -> p (a b)").bitcast(F32)

    stack = ExitStack()
    with stack:
        P = Prog(nc, stack)
        block = stack.enter_context(nc.Block())

        bX = [Buf("X%d" % i) for i in range(NT)]
        bHT = [Buf("HT%d" % i) for i in range(NT)]
        bGB = Buf(); bHN = [Buf(), Buf()]; bCONST = Buf(); bSS = Buf(); bRSTD = Buf()
        bXH = Buf(); bHNH = Buf(); bHTH = Buf(); bPST = Buf()
        bPS = [Buf("ps%d" % i) for i in range(7)]
        ps_next = [0]
        ps_pool = [list(range(7))]
        acc_next = [0]

        def psum():
            pool = ps_pool[0]
            i = pool[ps_next[0] % len(pool)]
            ps_next[0] += 1
            if i == 7:
                return PST_F32, bPST
            return PSB[i], bPS[i]

        def psum_acc():
            i = 5 + (acc_next[0] % 2)
            acc_next[0] += 1
            return PSB[i], bPS[i]

        g0row = (conv_norm if layers[0] % 2 == 0 else attn_norm)[layers[0] // 2]
        P.dma("sp", GB[:], bass.AP(g0row.tensor, g0row.offset, [[0, 128], [1, D_MODEL]]), wr=[bGB])
        gb_preloaded = [True]
        xv = x_in.rearrange("(i p) d -> p i d", p=128)
        for i in range(NT):
            P.dma("sp", X[:, i, :], xv[:, i, :], wr=[bX[i]])
        if layers[0] % 2 == 0:
            P.dma("sp", XH[:], xh_in, wr=[bXH])
        ctk = []
        ctk.append(P.dma("pool", IDENT[:], ident_in))
        ctk.append(P.dma("pool", ONESBD[:], onesbd_in))
        bFLAG = Buf()
        P.dma("sp", FLAG[:], flag_in, wr=[bFLAG])
        if FORCE_NEGF:
            P.memset("pool", NEGF[:], -30000.0, wr=[bFLAG])
        else:
            P.op("dve", lambda e: e.tensor_scalar(out=NEGF[:], in0=FLAG[:], scalar1=-1.0, scalar2=30000.0, op0=ALU.add, op1=ALU.mult),
                 rd=[bFLAG], wr=[bFLAG])
        P.op("pool", lambda e: e.memset(EPSC[:], EPS), extra=ctk, wr=[bCONST])
        cwk, gqk = [], []
        for jl in range(2):
            if (2 * jl) in layers:
                for kt in range(3):
                    cwk.append(P.dma("sp", c_CW[:, jl, :, kt], conv_w[jl, kt].rearrange("(j p) -> p j", p=128),
                                     allow_slow_non_contiguous=True))
            if (2 * jl + 1) in layers:
                for half in range(2):
                    gqk.append(P.dma("sp", a_GQK[half * 64:(half + 1) * 64, jl, 0:3], attn_q_gain[jl].rearrange("g d -> d g"),
                                     allow_slow_non_contiguous=True))
                    gqk.append(P.dma("sp", a_GQK[half * 64:(half + 1) * 64, jl, 3:6], attn_k_gain[jl].rearrange("g d -> d g"),
                                     allow_slow_non_contiguous=True))
        bCW = Buf(); bGQK = Buf()
        P.op("pool", lambda e: e.memset(JOINT[:, 0:1], 0.0), extra=cwk, wr=[bCW])
        P.op("pool", lambda e: e.memset(JOINT[:, 1:2], 0.0), extra=gqk, wr=[bGQK])

        def norm_rows(xap, np_, hn, bx, bhn, col):
            ssc = SS[0:np_, col:col + 1]
            rsc = RSTD[0:np_, col:col + 1]
            P.act(hn[0:np_, :], xap, AF.Square, rd=[bx], wr=[bhn, bSS], accum_out=ssc)
            P.act(rsc, ssc, AF.Ln, rd=[bSS, bCONST], wr=[bRSTD], scale=1.0 / D_MODEL, bias=EPSC[0:np_, :])
            P.act(rsc, rsc, AF.Exp, rd=[bRSTD], wr=[bRSTD], scale=-0.5)
            P.stt("dve", hn[0:np_, :], xap, rsc, GB[0:np_, :], ALU.mult, ALU.mult, rd=[bx, bRSTD, bGB], wr=[bhn])

        def norm_phase(gain_dram_row, with_halo):
            gsrc = bass.AP(gain_dram_row.tensor, gain_dram_row.offset, [[0, 128], [1, D_MODEL]])
            if gb_preloaded[0]:
                gb_preloaded[0] = False
            else:
                P.dma("sp", GB[:], gsrc, wr=[bGB])
            P.memset("pool", SS[:], 0.0, wr=[bSS])
            for i in range(NT):
                hn = HN[i % 2]
                norm_rows(X[:, i, :], 128, hn, bX[i], bHN[i % 2], i)
                for c in range(8):
                    P.tr(PST[:, c, :], hn[:, c * 128:(c + 1) * 128], IDENT[:], rd=[bHN[i % 2], bCONST], wr=[bPST], sig=(c == 7))
                P.cp("dve", HT[:, :, i * 128:(i + 1) * 128], PST[:], rd=[bPST], wr=[bHT[i]])
            if with_halo:
                norm_rows(XH[:], 2, HNH, bXH, bHNH, NT)
                for c in range(8):
                    P.tr(PST[:, c, 0:2], HNH[0:2, c * 128:(c + 1) * 128], IDENT[0:2, 0:2], rd=[bHNH, bCONST], wr=[bPST], sig=(c == 7))
                P.acopy(HTH[:], PST[:, :, 0:2], rd=[bPST], wr=[bHTH])

        def out_proj(YT, bYT, WO, bWO, nk, first_last):
            order = list(range(NT))
            if first_last:
                order = [NT - 1] + list(range(NT - 1))
            for i in order:
                for nh in range(2):
                    po, bpo = psum()
                    for kk in range(nk):
                        P.mm(po[:, :], YT[:, kk, i * 128:(i + 1) * 128], WO[:, kk, nh * 512:(nh + 1) * 512],
                             kk == 0, kk == nk - 1, rd=[bWO] + bYT, wr=[bpo], sig=(kk == nk - 1))
                    xs = X[:, i, nh * 512:(nh + 1) * 512]
                    P.tt("dve", xs, po[:, :], xs, ALU.add, rd=[bpo], wr=[bX[i]])

        def conv_layer(jl):
            winv = conv_w_in[jl].rearrange("(k p) (s j c) -> p k s j c", p=128, s=4, j=16, c=128)
            woutv = conv_w_out[jl].rearrange("(r kk p) n -> p r kk n", p=128, kk=4)
            bWJ = [Buf(), Buf()]; bWO = [Buf(), Buf()]; bYT = [Buf() for _ in range(4)]
            bVV = [Buf(), Buf()]; bCS = [Buf(), Buf()]; bSZ = [Buf(), Buf()]; bT1 = [Buf(), Buf()]
            bCH = Buf()

            def load_wj(j):
                for s in range(4):
                    P.dma("pool", c_WJ[j % 2][:, :, s, :], winv[:, :, s, j, :], wr=[bWJ[j % 2]])

            def load_wo(r):
                for kk in range(4):
                    P.dma("pool", c_WO[r % 2][:, kk, :], woutv[:, r, kk, :], wr=[bWO[r % 2]])

            load_wj(0)
            norm_phase(conv_norm[jl], True)
            it = 0
            for j in range(16):
                if j + 1 < 16:
                    load_wj(j + 1)
                if j % 4 == 0:
                    load_wo(j // 4)
                WJ = c_WJ[j % 2]
                bw = bWJ[j % 2]
                if j == 3:
                    while deferred:
                        deferred.pop(0)()
                ph, bph = psum()
                for sec in (1, 2):
                    for k in range(8):
                        P.mm(ph[:, (sec - 1) * 2:(sec - 1) * 2 + 2], WJ[:, k, sec, :], HTH[:, k, :], k == 0, k == 7,
                             rd=[bw, bHTH], wr=[bph], sig=(sec == 2 and k == 7))
                P.acopy(c_CH[:], ph[:, 0:2], rd=[bph], wr=[bCH])
                for s in range(4):
                    vb = it % 2
                    it += 1
                    VV = c_VV[vb]; CS = c_CS[vb]; SZ = c_SZ[vb]; T1 = c_T1[vb]
                    pss = {}
                    for sec in (1, 2, 3, 0):
                        pq, bpq = psum()
                        pss[sec] = (pq, bpq)
                        for k in range(8):
                            P.mm(pq[:, :], WJ[:, k, sec, :], HT[:, k, s * 512:(s + 1) * 512], k == 0, k == 7,
                                 rd=[bw] + bHT[4 * s:4 * s + 4], wr=[bpq], sig=(k == 7))
                    pc, bpc = pss[1]; pu, bpu = pss[2]; pz, bpz = pss[3]; pb, bpb = pss[0]
                    P.acopy(CS[:], pc[:, :], rd=[bpc], wr=[bCS[vb]])
                    if s == 0:
                        P.tt("dve", VV[:, 0:2], c_CH[:], ph[:, 2:4], ALU.mult, rd=[bCH, bph], wr=[bVV[vb]])
                    else:
                        P.cp("dve", VV[:, 0:2], c_VV[1 - vb][:, 512:514], rd=[bVV[1 - vb]], wr=[bVV[vb]])
                    P.tt("dve", VV[:, 2:514], CS[:], pu[:, :], ALU.mult, rd=[bCS[vb], bpu], wr=[bVV[vb]])
                    P.act(SZ[:], pz[:, :], AF.Silu, rd=[bpz], wr=[bSZ[vb]])
                    P.ts("dve", T1[:], VV[:, 2:514], c_CW[:, jl, j, 2:3], ALU.mult, rd=[bVV[vb], bCW], wr=[bT1[vb]])
                    P.stt("dve", T1[:], VV[:, 1:513], c_CW[:, jl, j, 1:2], T1[:], ALU.mult, ALU.add, rd=[bVV[vb], bT1[vb]], wr=[bT1[vb]])
                    P.stt("dve", T1[:], VV[:, 0:512], c_CW[:, jl, j, 0:1], T1[:], ALU.mult, ALU.add, rd=[bVV[vb], bT1[vb]], wr=[bT1[vb]])
                    P.tt("dve", T1[:], T1[:], pb[:, :], ALU.mult, rd=[bT1[vb], bpb], wr=[bT1[vb]])
                    P.tt("pool", c_YT[:, j % 4, s * 512:(s + 1) * 512], T1[:], SZ[:], ALU.mult, rd=[bT1[vb], bSZ[vb]], wr=[bYT[j % 4]])
                if j % 4 == 3:
                    r = j // 4
                    out_proj(c_YT, bYT, c_WO[r % 2], bWO[r % 2], 4, r == 3)

        def bias_setup():
            bRB = Buf(); bOH = Buf(); bF = Buf(); bFD = Buf(); bFT = Buf()
            P.memset("pool", a_RB[:], 0.0, wr=[bRB])
            P.memset("pool", a_RB[32:33, :], 1.0, wr=[bRB])
            P.dma("sp", a_RB[0:32, :], rel_bias, wr=[bRB])
            P.dma("sp", a_OH[:], oh_in, wr=[bOH])
            for g in range(3):
                ps, bps = psum()
                P.mm(ps[0:16, 0:FPER], a_RB[:, g * 16:(g + 1) * 16], a_OH[:, g, :], True, True, rd=[bRB, bOH], wr=[bps])
                P.cp("dve", a_F[:, g, :], ps[0:16, 0:FPER], rd=[bps], wr=[bF])
            P.dma("sp", FD.ap().rearrange("g h j -> h g j"), a_F[:], rd=[bF], wr=[bFD])
            for g in range(3):
                src = bass.AP(FD, g * 16 * FPER, [[FPER, 16], [0, 128], [1, FPER]])
                dst = bass.AP(FT, g * 16 * 128 * FPER, [[128 * FPER, 16], [FPER, 128], [1, FPER]])
                P.dma("sp", dst, src, rd=[bFD], wr=[bFT])
            return bFT

        def attn_layer(jl, l, bFT):
            win = attn_w_in[jl]
            wqkv = win[:, 0:9216].rearrange("(k p) (g w h c) -> p k g w h c", p=128, g=3, w=3, h=8, c=128)
            wz = win[:, 9216:10240].rearrange("(k p) (h c) -> p k h c", p=128, c=128)
            woutv = attn_w_out[jl].rearrange("(r kk p) n -> p r kk n", p=128, kk=2)
            bWA = [Buf(), Buf()]; bWZ = [Buf()] * 2; bWO = Buf(); bYT = [Buf(), Buf()]
            bACC = Buf(); bSZ = Buf(); bQT = Buf(); bKT = Buf(); bVB = Buf()
            bKTH = [Buf() for _ in range(3)]; bVBH = [Buf() for _ in range(3)]
            bBT = [Buf(), Buf()]; bSQ = [Buf(), Buf()]; bRS = [Buf(), Buf()]; bE = [Buf() for _ in range(4)]
            bPT = [Buf() for _ in range(8)]
            bPTH = [Buf() for _ in range(6)]
            bR = [Buf(), Buf()]; bTMP = [Buf(), Buf()]

            def load_wa(hp, g, buf):
                for w in range(3):
                    P.dma("pool", a_WA[buf][:, :, w, :], wqkv[:, :, g, w, hp, :], wr=[bWA[buf]])

            def load_bt(hp, g, buf):
                src = bass.AP(FT, ((g * 16 + 2 * hp) * 128 * FPER) + 127, [[FPER - 1, 128], [128 * FPER, 2], [1, 256]])
                P.dma("sp", a_BT[buf][:], src, rd=[bFT], wr=[bBT[buf]])

            load_wa(0, 0, 0)
            norm_phase(attn_norm[jl], False)
            P.memset("pool", a_VB[:, :, 64:128], 1.0, wr=[bVB])
            e_next = [0]
            pending_out = [False]
            pt_next = [0]
            pth_next = [0]
            it = 0
            for hp in range(8):
                P.dma("pool", a_WZ[hp % 2][:], wz[:, :, hp, :], wr=[bWZ[hp % 2]])
                def z_chunk(s):
                    ps, bps = psum()
                    for k in range(8):
                        P.mm(ps[:, :], a_WZ[hp % 2][:, k, :], HT[:, k, s * 512:(s + 1) * 512], k == 0, k == 7,
                             rd=[bWZ[hp % 2]] + bHT[4 * s:4 * s + 4], wr=[bps], sig=(k == 7))
                    P.act(a_SZ[:, s * 512:(s + 1) * 512], ps[:, :], AF.Silu, rd=[bps], wr=[bSZ])
                z_plan = {0: (0, 1), 1: (2,), 2: (3,)}
                for g, dil in enumerate(GROUP_DIL):
                    wbuf = it % 2
                    it += 1
                    nb = 16 // dil
                    WA = a_WA[wbuf]
                    if not (hp == 7 and g == 2):
                        nhp, ng = (hp, g + 1) if g < 2 else (hp + 1, 0)
                        load_wa(nhp, ng, 1 - wbuf)
                    if hp % 2 == 0 and g == 2:
                        for kk in range(2):
                            P.dma("pool", a_WO[:, kk, :], woutv[:, hp // 2, kk, :], wr=[bWO])
                    load_bt(hp, g, wbuf)
                    BT = a_BT[wbuf]

                    def qk_proj(w, dest, bdest, gcol):
                        dvr = dest[:].rearrange("p (r s) -> p r s", r=dil)
                        n_s = 512 // dil
                        st = {}

                        def stage_p(s):
                            ps, bps = psum()
                            for k in range(8):
                                P.mm(ps[:, :], WA[:, k, w, :], HT[:, k, s * 512:(s + 1) * 512], k == 0, k == 7,
                                     rd=[bWA[wbuf]] + bHT[4 * s:4 * s + 4], wr=[bps], sig=(k == 7))
                            sb = s % 2
                            P.act(a_SQ[sb][:], ps[:, :], AF.Square, rd=[bps], wr=[bSQ[sb]])
                            st[s] = (ps, bps)

                        def stage_o(s):
                            ps, bps = st.pop(s)
                            sb = s % 2
                            ps2, bps2 = psum()
                            P.mm(ps2[:, :], ONESBD[:], a_SQ[sb][:], True, True, rd=[bSQ[sb], bCONST], wr=[bps2])
                            P.act(a_RS[sb][:], ps2[:, :], AF.Ln, rd=[bps2, bCONST], wr=[bRS[sb]], scale=1.0 / 64, bias=EPSC[:, :])
                            P.act(a_RS[sb][:], a_RS[sb][:], AF.Exp, rd=[bRS[sb]], wr=[bRS[sb]], scale=-0.5)
                            P.stt("dve", dvr[:, :, s * n_s:(s + 1) * n_s], ps[:, :].rearrange("p (s r) -> p r s", r=dil),
                                  a_GQK[:, jl, gcol:gcol + 1], a_RS[sb][:].rearrange("p (s r) -> p r s", r=dil),
                                  ALU.mult, ALU.mult, rd=[bps, bRS[sb], bGQK], wr=[bdest])

                        stage_p(0)
                        for s in range(1, 4):
                            stage_p(s)
                            stage_o(s - 1)
                        return lambda: stage_o(3)

                    k_tail = qk_proj(1, a_KT, bKT, 3 + g)
                    for qd in range(4):
                        if qd == 1:
                            k_tail()
                        ps, bps = psum()
                        for bb in range(4):
                            blk = qd * 4 + bb
                            r, n = blk // nb, blk % nb
                            t0 = r + dil * 128 * n
                            for k in range(8):
                                P.mm(ps[:, bb * 128:(bb + 1) * 128], HT[:, k, t0:t0 + dil * 127 + 1:dil], WA[:, k, 2, :], k == 0, k == 7,
                                     rd=[bWA[wbuf]] + bHT, wr=[bps], sig=(bb == 3 and k == 7))
                        dst = sb_ap(a_VB, qd * 4 * 192, [[192, 4], [128, 2], [1, 64]])
                        P.acopy(dst, ps[:, :].rearrange("p (b h d) -> p b h d", b=4, h=2), rd=[bps], wr=[bVB])
                    if pending_out[0]:
                        pending_out[0] = False
                        out_proj(a_YT, bYT, a_WO, bWO, 2, False)
                    sndk, sndv = SND[(l, hp, g)]; rcvk, rcvv = RCV[(l, hp, g)]
                    bSk = Buf(); bSv = Buf(); bRk = Buf(); bRv = Buf()
                    ksrc = sb_ap(a_KT, (nb - 1) * 128, [[nb * 128, dil], [1, 128]])
                    P.dma("sp", sndk.ap().rearrange("p (r c) -> p r c", c=128), ksrc, rd=[bKT], wr=[bSk])
                    vsrc = sb_ap(a_VB, (nb - 1) * 192, [[nb * 192, dil], [1, 192]])
                    P.dma("sp", sndv.ap().rearrange("p (r c) -> p r c", c=192), vsrc, rd=[bVB], wr=[bSv])
                    P.collective(lambda e, a=sndk, b_=rcvk: e.collective_compute(
                        "AllGather", ALU.bypass, replica_groups=PAIRS, ins=[a.ap().opt()], outs=[b_.ap().opt()]),
                        rd=[bSk], wr=[bRk])
                    P.collective(lambda e, a=sndv, b_=rcvv: e.collective_compute(
                        "AllGather", ALU.bypass, replica_groups=PAIRS, ins=[a.ap().opt()], outs=[b_.ap().opt()]),
                        rd=[bSv], wr=[bRv])
                    P.dma("sp", a_KTH[g][:], rcvk.ap()[0:128, :], rd=[bRk], wr=[bKTH[g]])
                    P.dma("sp", a_VBH[g][:], rcvv.ap()[0:128, :].rearrange("p (r c) -> p r c", c=192), rd=[bRv], wr=[bVBH[g]])
                    q_tail = qk_proj(0, a_QT, bQT, g)
                    zc = z_plan[g]
                    z_chunk(zc[0])
                    q_tail()
                    for s_ in zc[1:]:
                        z_chunk(s_)
                    LA = 5
                    assert len(a_PT) >= LA + 3 and len(a_PTH) >= LA + 1

                    def score(head, kt_ap, bkt, q0, N, btc0, ptile, bpt, halo=False):
                        rows = slice(64 * head, 64 * head + 64)
                        ps, bps = psum()
                        P.mm(ps[:, 0:N], kt_ap, a_QT[rows, q0:q0 + N], True, True, rd=[bkt, bQT], wr=[bps])
                        ei = e_next[0]
                        e_next[0] = (ei + 1) % len(a_E)
                        P.stt("dve", a_E[ei][:, 0:N], ps[:, 0:N], 0.125, BT[:, head, btc0:btc0 + N], ALU.mult, ALU.add,
                              rd=[bps, bBT[wbuf]], wr=[bE[ei]])
                        if halo:
                            P.act(ptile[:, 0:N], a_E[ei][:, 0:N], AF.Exp, rd=[bE[ei], bFLAG], wr=[bpt], bias=NEGF[:, :])
                        else:
                            P.act(ptile[:, 0:N], a_E[ei][:, 0:N], AF.Exp, rd=[bE[ei]], wr=[bpt])

                    tasks = [(head, b) for b in range(16) for head in range(2)]
                    slot = {}
                    hslot = {}
                    acc_banks = {}

                    def emit_scores(head, b):
                        rows = slice(64 * head, 64 * head + 64)
                        r, n = b // nb, b % nb
                        N = 256 if n < nb - 1 else 128
                        sl = pt_next[0] % len(a_PT)
                        pt_next[0] += 1
                        slot[(head, b)] = sl
                        score(head, a_KT[rows, b * 128:(b + 1) * 128], bKT, b * 128, N, 0, a_PT[sl], bPT[sl])
                        if n == 0 and not NOHALO:
                            hs = pth_next[0] % len(a_PTH)
                            pth_next[0] += 1
                            hslot[(head, b)] = hs
                            score(head, a_KTH[g][rows, r * 128:(r + 1) * 128], bKTH[g], b * 128, 128, 128, a_PTH[hs], bPTH[hs], halo=True)

                    def emit_pv(head, b):
                        vc = slice(64 * head, 64 * head + 128)
                        r, n = b // nb, b % nb
                        qd, bb = b // 4, b % 4
                        if bb == 0:
                            acc_banks[(head, qd)] = psum_acc()
                        po, bpo = acc_banks[(head, qd)]
                        pob = po[:, bb * 128:(bb + 1) * 128]
                        sl = slot[(head, b)]
                        if n == 0 and NOHALO:
                            pass
                        elif n == 0:
                            hs = hslot[(head, b)]
                            P.mm(pob, a_VBH[g][:, r, vc], a_PTH[hs][:, 0:128], True, False, rd=[bVBH[g], bPTH[hs]], wr=[bpo], sig=False)
                        else:
                            sp_ = slot[(head, b - 1)]
                            P.mm(pob, a_VB[:, b - 1, vc], a_PT[sp_][:, 128:256], True, False, rd=[bVB, bPT[sp_]], wr=[bpo], sig=False)
                        P.mm(pob, a_VB[:, b, vc], a_PT[sl][:, 0:128], (n == 0 and NOHALO), True, rd=[bVB, bPT[sl]], wr=[bpo], sig=(bb == 3))
                        if bb == 3:
                            b0 = qd * 4
                            if dil == 1:
                                acc = sb_ap(a_ACC, head * T + b0 * 128, [[1, 512]])
                                pin = po[:, :]
                            elif dil == 4:
                                acc = sb_ap(a_ACC, head * T + qd, [[4, 512]])
                                pin = po[:, :]
                            else:
                                acc = sb_ap(a_ACC, head * T + b0, [[1, 4], [16, 128]])
                                pin = po[:, :].rearrange("p (b i) -> p b i", b=4)
                            if g == 0:
                                P.cp("dve", acc, pin, rd=[bpo], wr=[bACC])
                            else:
                                P.tt("dve", acc, pin, acc, ALU.add, rd=[bpo], wr=[bACC])

                    for i in range(len(tasks) + LA):
                        if i < len(tasks):
                            emit_scores(*tasks[i])
                        if i >= LA:
                            emit_pv(*tasks[i - LA])
                for c in range(8):
                    cb = c % 2
                    cs = slice(c * 256, (c + 1) * 256)
                    P.act(a_R[cb][0:64, :], a_ACC[64:128, 0, cs], AF.Ln, rd=[bACC], wr=[bR[cb]])
                    P.act(a_R[cb][64:128, :], a_ACC[0:64, 1, cs], AF.Ln, rd=[bACC], wr=[bR[cb]])
                    P.act(a_R[cb][:, :], a_R[cb][:, :], AF.Exp, rd=[bR[cb]], wr=[bR[cb]], scale=-1.0)
                    P.tt("dve", a_TMP[cb][0:64, :], a_ACC[0:64, 0, cs], a_R[cb][0:64, :], ALU.mult, rd=[bACC, bR[cb]], wr=[bTMP[cb]])
                    P.tt("dve", a_TMP[cb][64:128, :], a_ACC[64:128, 1, cs], a_R[cb][64:128, :], ALU.mult, rd=[bACC, bR[cb]], wr=[bTMP[cb]])
                    P.tt("pool", a_YT[:, hp % 2, cs], a_TMP[cb][:], a_SZ[:, cs], ALU.mult, rd=[bTMP[cb], bSZ], wr=[bYT[hp % 2]])
                if hp % 2 == 1:
                    if hp == 7:
                        out_proj(a_YT, bYT, a_WO, bWO, 2, True)
                    else:
                        pending_out[0] = True

        bFT_box = [None]
        deferred = []
        if has_attn:
            if layers[0] % 2 == 1:
                bFT_box[0] = bias_setup()
                P.barrier()
            else:
                deferred.append(lambda: bFT_box.__setitem__(0, bias_setup()))
        for li, l in enumerate(layers):
            if li > 0:
                P.barrier()
            if l % 2 == 0:
                if li > 0:
                    bSX = Buf(); bRX = Buf()
                    P.dma("sp", SNDX.ap(), X[126:128, NT - 1, :], rd=[bX[NT - 1]], wr=[bSX])
                    P.collective(lambda e: e.collective_compute("AllGather", ALU.bypass, replica_groups=PAIRS,
                                                                ins=[SNDX.ap().opt()], outs=[RCVX.ap().opt()]), rd=[bSX], wr=[bRX])
                    P.dma("sp", XH[:], RCVX.ap()[0:2, :], rd=[bRX], wr=[bXH])
                    P.ts("dve", XH[:], XH[:], FLAG[0:2, 0:1], ALU.mult, rd=[bFLAG], wr=[bXH])
                ps_pool[0] = list(range(7))
                conv_layer(l // 2)
            else:
                ps_pool[0] = [0, 1, 2, 3, 4, 7]
                attn_layer(l // 2, l, bFT_box[0])

        yv = y_out.rearrange("(i p) d -> p i d", p=128)
        tks = []
        for i in range(NT):
            tks.append(P.dma("sp", yv[:, i, :], X[:, i, :], rd=[bX[i]]))
        P.wait_all("sp", tks)
        P.emit(block)
    return nc


_CONST = {}


def _consts():
    if not _CONST:
        bd = np.zeros((128, 128), np.float32)
        bd[0:64, 0:64] = 1.0
        bd[64:128, 64:128] = 1.0
        _CONST["ident"] = np.eye(128, dtype=np.float32)
        _CONST["onesbd"] = bd
        _CONST["oh"] = bias_onehot()
    return _CONST


_PROGS = {}


def _run(layers, x_shards, xh_shards, weights):
    key = tuple(layers)
    if key not in _PROGS:
        _PROGS[key] = build(list(layers))
    nc = _PROGS[key]
    c = _consts()
    in_maps = []
    for core in range(NCORES):
        m = dict(weights)
        m["x"] = x_shards[core]
        m["xh"] = xh_shards[core]
        m["flag"] = np.full((128, 1), float(core % 2), np.float32)
        m["ident"] = c["ident"]
        m["onesbd"] = c["onesbd"]
        m["oh"] = c["oh"]
        in_maps.append(m)
    res = run_bass_kernel_spmd(nc, in_maps, core_ids=list(range(NCORES)))
    return [r["y"] for r in res.results]


def _shard(x):
    xs, xh = [], []
    for core in range(NCORES):
        b, half = core // 2, core % 2
        xs.append(np.ascontiguousarray(x[b, half * T:(half + 1) * T, :]))
        if half == 0:
            xh.append(np.zeros((2, D_MODEL), np.float32))
        else:
            xh.append(np.ascontiguousarray(x[b, T - 2:T, :]))
    return xs, xh


def _unshard(ys):
    out = np.empty((4, 2 * T, D_MODEL), np.float32)
    for core in range(NCORES):
        b, half = core // 2, core % 2
        out[b, half * T:(half + 1) * T, :] = ys[core]
    return out


def kernel(x, conv_norm, conv_w_in, conv_w, conv_w_out, attn_norm, attn_w_in,
           attn_q_gain, attn_k_gain, attn_w_out, rel_bias, _layers=None):
    f = lambda a: np.ascontiguousarray(np.asarray(a, dtype=np.float32))
    weights = dict(conv_norm=f(conv_norm), conv_w_in=f(conv_w_in), conv_w=f(conv_w), conv_w_out=f(conv_w_out),
                   attn_norm=f(attn_norm), attn_w_in=f(attn_w_in), attn_q_gain=f(attn_q_gain),
                   attn_k_gain=f(attn_k_gain), attn_w_out=f(attn_w_out), rel_bias=f(rel_bias))
    x = f(x)
    if _layers is not None:
        groups = _layers
    elif FUSED:
        groups = [[0, 1, 2, 3]]
    else:
        groups = [[0], [1], [2], [3]]
    for grp in groups:
        xs, xh = _shard(x)
        ys = _run(grp, xs, xh, weights)
        x = _unshard(ys)
    return x
```

```python
import math
import numpy as np
import concourse.bass as bass
import concourse.mybir as mybir
from concourse.bass_utils import run_bass_kernel_spmd

F32 = mybir.dt.float32
BF16 = mybir.dt.bfloat16
AF = mybir.ActivationFunctionType
ALU = mybir.AluOpType

NCORES = 8
T = 2048
NT = T // 128
D_MODEL = 1024
EPS = 1e-6
GROUP_DIL = (1, 4, 16)
FUSED = True
import os
NOHALO = bool(int(os.environ.get('NOHALO', '0')))
FORCE_NEGF = bool(int(os.environ.get('FORCE_NEGF', '0')))
PAIRS = [[0, 1], [2, 3], [4, 5], [6, 7]]
FPER = 382


class Buf:
    __slots__ = ("w", "r", "name")

    def __init__(self, name=""):
        self.w = None
        self.r = {}
        self.name = name


class Prog:
    ENG = ("pe", "act", "dve", "pool", "sp")

    def __init__(self, nc, stack):
        self.nc = nc
        self.ops = {e: [] for e in self.ENG}
        self.sem = {}
        self.cnt = {}
        self.waited = {e: {} for e in self.ENG}
        for e in self.ENG:
            self.sem[e] = stack.enter_context(nc.semaphore("s_" + e))
            self.cnt[e] = 0
        self.ndma = 24
        self.dsem = []
        for q in ("sp", "pool"):
            for i in range(self.ndma):
                nm = "d_%s_%d" % (q, i)
                self.sem[nm] = stack.enter_context(nc.semaphore(nm))
                self.cnt[nm] = 0
        self.dnext = {"sp": 0, "pool": 0}
        self.sem["cc"] = stack.enter_context(nc.semaphore("s_cc"))
        self.cnt["cc"] = 0

    def _waits(self, eng, rd, wr, extra):
        need = {}

        def add(t):
            if t is None:
                return
            s, v = t
            if s == eng and v > self.cnt[eng]:
                return
            if need.get(s, 0) < v:
                need[s] = v
        for b in rd:
            add(b.w)
        for b in wr:
            add(b.w)
            for t in b.r.values():
                add(t)
        for t in extra:
            add(t)
        out = []
        wd = self.waited[eng]
        for s, v in need.items():
            if wd.get(s, 0) < v:
                wd[s] = v
                out.append((s, v))
        return out

    def _commit(self, ticket, rd, wr):
        for b in rd:
            b.r[ticket[0]] = ticket
        for b in wr:
            b.w = ticket
            b.r = {}

    def op(self, eng, fn, rd=(), wr=(), extra=(), sig=True):
        waits = self._waits(eng, rd, wr, extra)
        if sig:
            self.cnt[eng] += 1
            tk = (eng, self.cnt[eng])
            self.ops[eng].append((waits, fn, (eng, 1)))
        else:
            tk = (eng, self.cnt[eng] + 1)
            self.ops[eng].append((waits, fn, None))
        self._commit(tk, rd, wr)
        return tk

    def dma(self, q, out, in_, rd=(), wr=(), extra=(), **kw):
        i = self.dnext[q]
        self.dnext[q] = (i + 1) % self.ndma
        nm = "d_%s_%d" % (q, i)
        prev = (nm, self.cnt[nm]) if self.cnt[nm] > 0 else None
        waits = self._waits(q, rd, wr, list(extra) + [prev])
        self.cnt[nm] += 16
        tk = (nm, self.cnt[nm])
        self.ops[q].append((waits, lambda e: e.dma_start(out=out, in_=in_, **kw), (nm, 16)))
        self._commit(tk, rd, wr)
        return tk

    def collective(self, fn, rd=(), wr=()):
        waits = self._waits("pool", rd, wr, [("cc", self.cnt["cc"])] if self.cnt["cc"] else [])
        self.cnt["cc"] += 1
        tk = ("cc", self.cnt["cc"])
        self.ops["pool"].append((waits, fn, ("cc", 1)))
        self._commit(tk, rd, wr)
        return tk

    def barrier(self):
        tk = [(s, c) for s, c in self.cnt.items() if c > 0]
        for e in self.ENG:
            self.wait_all(e, tk)

    def mm(self, out, lhsT, rhs, start, stop, rd, wr, sig=True):
        return self.op("pe", lambda e: e.matmul(out, lhsT, rhs, start=start, stop=stop), rd=rd, wr=wr, sig=sig)

    def tr(self, out, in_, ident, rd, wr, sig=True):
        return self.op("pe", lambda e: e.transpose(out, in_, ident), rd=rd, wr=wr, sig=sig)

    def act(self, out, in_, func, rd, wr, **kw):
        return self.op("act", lambda e: e.activation(out=out, in_=in_, func=func, **kw), rd=rd, wr=wr)

    def acopy(self, out, in_, rd, wr):
        return self.op("act", lambda e: e.copy(out=out, in_=in_), rd=rd, wr=wr)

    def tt(self, eng, out, in0, in1, op, rd, wr):
        return self.op(eng, lambda e: e.tensor_tensor(out=out, in0=in0, in1=in1, op=op), rd=rd, wr=wr)

    def stt(self, eng, out, in0, scalar, in1, op0, op1, rd, wr):
        return self.op(eng, lambda e: e.scalar_tensor_tensor(out=out, in0=in0, scalar=scalar, in1=in1, op0=op0, op1=op1), rd=rd, wr=wr)

    def ts(self, eng, out, in0, scalar1, op0, rd, wr):
        return self.op(eng, lambda e: e.tensor_scalar(out=out, in0=in0, scalar1=scalar1, scalar2=None, op0=op0), rd=rd, wr=wr)

    def cp(self, eng, out, in_, rd, wr):
        return self.op(eng, lambda e: e.tensor_copy(out=out, in_=in_), rd=rd, wr=wr)

    def recip(self, out, in_, rd, wr):
        return self.op("dve", lambda e: e.reciprocal(out=out, in_=in_), rd=rd, wr=wr)

    def memset(self, eng, ap, val, wr):
        return self.op(eng, lambda e: e.memset(ap, val), wr=wr)

    def wait_all(self, eng, tickets):
        waits = self._waits(eng, (), (), tickets)
        self.ops[eng].append((waits, None, None))

    def emit(self, block):
        sem = self.sem

        def mk(eng):
            lst = self.ops[eng]

            def body(e):
                for waits, fn, inc in lst:
                    for s, v in waits:
                        e.wait_ge(sem[s], v)
                    if fn is None:
                        continue
                    r = fn(e)
                    if inc is not None:
                        r.then_inc(sem[inc[0]], inc[1])
            return body
        block.tensor(mk("pe"))
        block.scalar(mk("act"))
        block.vector(mk("dve"))
        block.gpsimd(mk("pool"))
        block.sync(mk("sp"))


def sb_ap(t, off, dims):
    base = t[:]
    p = base.ap[0]
    return bass.AP(t, off, [[p[0], p[1]]] + [list(d) for d in dims])


def sb_ap_p(t, p0, np_, off, dims):
    base = t[p0:p0 + np_]
    p = base.ap[0]
    return bass.AP(t, base.offset + off, [[p[0], p[1]]] + [list(d) for d in dims])


def _t5_bucket_np(dist):
    dist = np.asarray(dist, dtype=np.int64)
    d = np.maximum(dist, 1).astype(np.float32)
    large = 16 + (np.log(d / np.float32(16)) / np.float32(math.log(2048 / 16)) * np.float32(16)).astype(np.int32)
    large = np.minimum(large, 31)
    return np.where(dist < 16, dist, large).astype(np.int64)


def bias_onehot():
    oh = np.zeros((64, 3, FPER), np.float32)
    for g, dil in enumerate(GROUP_DIL):
        for j in range(FPER):
            step = j - 127
            if 0 <= step <= 128:
                b = int(_t5_bucket_np(np.array([step * dil]))[0])
                oh[b, g, j] = 1.0
            else:
                oh[32, g, j] = -30000.0
    return oh


def build(layers):
    from contextlib import ExitStack
    nc = bass.Bass("TRN2", target_bir_lowering=False)
    dt = nc.dram_tensor
    x_in = dt("x", [T, D_MODEL], F32, kind="ExternalInput").ap()
    xh_in = dt("xh", [2, D_MODEL], F32, kind="ExternalInput").ap()
    flag_in = dt("flag", [128, 1], F32, kind="ExternalInput").ap()
    ident_in = dt("ident", [128, 128], F32, kind="ExternalInput").ap()
    onesbd_in = dt("onesbd", [128, 128], F32, kind="ExternalInput").ap()
    oh_in = dt("oh", [64, 3, FPER], F32, kind="ExternalInput").ap()
    conv_norm = dt("conv_norm", [2, 1024], F32, kind="ExternalInput").ap()
    conv_w_in = dt("conv_w_in", [2, 1024, 8192], F32, kind="ExternalInput").ap()
    conv_w = dt("conv_w", [2, 3, 2048], F32, kind="ExternalInput").ap()
    conv_w_out = dt("conv_w_out", [2, 2048, 1024], F32, kind="ExternalInput").ap()
    attn_norm = dt("attn_norm", [2, 1024], F32, kind="ExternalInput").ap()
    attn_w_in = dt("attn_w_in", [2, 1024, 10240], F32, kind="ExternalInput").ap()
    attn_q_gain = dt("attn_q_gain", [2, 3, 64], F32, kind="ExternalInput").ap()
    attn_k_gain = dt("attn_k_gain", [2, 3, 64], F32, kind="ExternalInput").ap()
    attn_w_out = dt("attn_w_out", [2, 1024, 1024], F32, kind="ExternalInput").ap()
    rel_bias = dt("rel_bias", [32, 48], F32, kind="ExternalInput").ap()
    y_out = dt("y", [T, D_MODEL], F32, kind="ExternalOutput").ap()

    has_attn = any(l % 2 == 1 for l in layers)
    has_conv = any(l % 2 == 0 for l in layers)
    FD = dt("fd", [3, 16, FPER], F32)
    FT = dt("ft", [3, 16, 128 * FPER], F32)
    SNDX = dt("sndx", [2, D_MODEL], F32)
    RCVX = dt("rcvx", [4, D_MODEL], F32)
    SND = {}
    RCV = {}
    for l in layers:
        if l % 2 == 1:
            for hp in range(8):
                for g, dil in enumerate(GROUP_DIL):
                    SND[(l, hp, g)] = (dt("sndk_%d_%d_%d" % (l, hp, g), [128, 128 * dil], BF16), dt("sndv_%d_%d_%d" % (l, hp, g), [128, 192 * dil], BF16))
                    RCV[(l, hp, g)] = (dt("rcvk_%d_%d_%d" % (l, hp, g), [256, 128 * dil], BF16), dt("rcvv_%d_%d_%d" % (l, hp, g), [256, 192 * dil], BF16))

    sbt = nc.alloc_sbuf_tensor
    X = sbt("X", [128, NT, D_MODEL], F32)
    HT = sbt("HT", [128, 8, T], BF16)
    GB = sbt("GB", [128, D_MODEL], F32)
    HN = [sbt("HN%d" % i, [128, D_MODEL], BF16) for i in range(2)]
    IDENT = sbt("IDENT", [128, 128], BF16)
    ONESBD = sbt("ONESBD", [128, 128], BF16)
    SS = sbt("SS", [128, NT + 1], F32)
    RSTD = sbt("RSTD", [128, NT + 1], F32)
    FLAG = sbt("FLAG", [128, 1], F32)
    HTH = sbt("HTH", [128, 8, 2], BF16)
    EPSC = sbt("EPSC", [128, 1], F32)
    JOINT = sbt("JOINT", [128, 2], F32)
    NEGF = sbt("NEGF", [128, 1], F32)
    c_CW = sbt("cCW", [128, 2, 16, 3], F32)
    a_GQK = sbt("aGQK", [128, 2, 6], F32)
    abase = (nc._sbuf_addr_for_side(None) + 63) // 64 * 64
    alimit = nc._sbuf_addr_for_side(None) + nc.sbuf_bytes_remaining
    esz = {F32: 4, BF16: 2}
    cur = [abase]

    def arena(name, shape, dtype):
        n = esz[dtype]
        for d in shape[1:]:
            n *= d
        n = (n + 63) // 64 * 64
        t = nc.alloc_sbuf_tensor_at(name, shape, dtype, offset=cur[0])
        cur[0] += n
        assert cur[0] <= alimit, ("SBUF arena overflow", name, cur[0], alimit)
        return t

    if has_conv:
        cur[0] = abase
        c_WJ = [arena("cWJ%d" % i, [128, 8, 4, 128], BF16) for i in range(2)]
        c_WO = [arena("cWO%d" % i, [128, 4, D_MODEL], BF16) for i in range(2)]
        c_YT = arena("cYT", [128, 4, T], BF16)
        c_VV = [arena("cVV%d" % i, [128, 514], F32) for i in range(2)]
        c_CS = [arena("cCS%d" % i, [128, 512], F32) for i in range(2)]
        c_SZ = [arena("cSZ%d" % i, [128, 512], F32) for i in range(2)]
        c_T1 = [arena("cT1%d" % i, [128, 512], F32) for i in range(2)]
        c_CH = arena("cCH", [128, 2], F32)
        XH = arena("XH", [2, D_MODEL], F32)
        HNH = arena("HNH", [2, D_MODEL], BF16)
        conv_end = cur[0]
    if has_attn:
        cur[0] = abase
        a_WA = [arena("aWA%d" % i, [128, 8, 3, 128], BF16) for i in range(2)]
        a_WZ = [arena("aWZ0", [128, 8, 128], BF16)] * 2
        a_WO = arena("aWO", [128, 2, D_MODEL], BF16)
        a_YT = arena("aYT", [128, 2, T], BF16)
        a_ACC = arena("aACC", [128, 2, T], F32)
        a_SZ = arena("aSZ", [128, T], BF16)
        a_QT = arena("aQT", [128, T], BF16)
        a_KT = arena("aKT", [128, T], BF16)
        a_VB = arena("aVB", [128, 16, 192], BF16)
        a_KTH = [arena("aKTH%d" % g, [128, 128 * d], BF16) for g, d in enumerate(GROUP_DIL)]
        a_VBH = [arena("aVBH%d" % g, [128, d, 192], BF16) for g, d in enumerate(GROUP_DIL)]
        a_BT = [arena("aBT%d" % i, [128, 2, 256], F32) for i in range(2)]
        a_SQ = [arena("aSQ%d" % i, [128, 512], BF16) for i in range(2)]
        a_RS = [arena("aRS%d" % i, [128, 512], F32) for i in range(2)]
        a_E = [arena("aE%d" % i, [128, 256], F32) for i in range(4)]
        a_PT = [arena("aPT%d" % i, [128, 256], BF16) for i in range(8)]
        a_PTH = [arena("aPTH%d" % i, [128, 128], BF16) for i in range(6)]
        a_R = [arena("aR%d" % i, [128, 256], F32) for i in range(2)]
        a_TMP = [arena("aTMP%d" % i, [128, 256], F32) for i in range(2)]
        boff = abase + 81920
        assert (not has_conv) or conv_end <= boff
        a_RB = nc.alloc_sbuf_tensor_at("aRB", [64, 48], F32, offset=boff)
        a_OH = nc.alloc_sbuf_tensor_at("aOH", [64, 3, FPER], F32, offset=boff + 256)
        a_F = nc.alloc_sbuf_tensor_at("aF", [16, 3, FPER], F32, offset=boff + 256 + 4608)
        assert boff + 256 + 2 * 4608 <= alimit
    PSB = [nc.alloc_psum_tensor("ps%d" % i, [128, 512], F32) for i in range(7)]
    PST = nc.alloc_psum_tensor("pst", [128, 8, 128], BF16)
    PST_F32 = PST[:].rearrange("p a b -> p (a b)").bitcast(F32)

    stack = ExitStack()
    with stack:
        P = Prog(nc, stack)
        block = stack.enter_context(nc.Block())

        bX = [Buf("X%d" % i) for i in range(NT)]
        bHT = [Buf("HT%d" % i) for i in range(NT)]
        bGB = Buf(); bHN = [Buf(), Buf()]; bCONST = Buf(); bSS = Buf(); bRSTD = Buf()
        bXH = Buf(); bHNH = Buf(); bHTH = Buf(); bPST = Buf()
        bPS = [Buf("ps%d" % i) for i in range(7)]
        ps_next = [0]
        ps_pool = [list(range(7))]
        acc_next = [0]

        def psum():
            pool = ps_pool[0]
            i = pool[ps_next[0] % len(pool)]
            ps_next[0] += 1
            if i == 7:
                return PST_F32, bPST
            return PSB[i], bPS[i]

        def psum_acc():
            i = 5 + (acc_next[0] % 2)
            acc_next[0] += 1
            return PSB[i], bPS[i]

        g0row = (conv_norm if layers[0] % 2 == 0 else attn_norm)[layers[0] // 2]
        P.dma("sp", GB[:], bass.AP(g0row.tensor, g0row.offset, [[0, 128], [1, D_MODEL]]), wr=[bGB])
        gb_preloaded = [True]
        xv = x_in.rearrange("(i p) d -> p i d", p=128)
        for i in range(NT):
            P.dma("sp", X[:, i, :], xv[:, i, :], wr=[bX[i]])
        if layers[0] % 2 == 0:
            P.dma("sp", XH[:], xh_in, wr=[bXH])
        ctk = []
        ctk.append(P.dma("pool", IDENT[:], ident_in))
        ctk.append(P.dma("pool", ONESBD[:], onesbd_in))
        bFLAG = Buf()
        P.dma("sp", FLAG[:], flag_in, wr=[bFLAG])
        if FORCE_NEGF:
            P.memset("pool", NEGF[:], -30000.0, wr=[bFLAG])
        else:
            P.op("dve", lambda e: e.tensor_scalar(out=NEGF[:], in0=FLAG[:], scalar1=-1.0, scalar2=30000.0, op0=ALU.add, op1=ALU.mult),
                 rd=[bFLAG], wr=[bFLAG])
        P.op("pool", lambda e: e.memset(EPSC[:], EPS), extra=ctk, wr=[bCONST])
        cwk, gqk = [], []
        for jl in range(2):
            if (2 * jl) in layers:
                for kt in range(3):
                    cwk.append(P.dma("sp", c_CW[:, jl, :, kt], conv_w[jl, kt].rearrange("(j p) -> p j", p=128),
                                     allow_slow_non_contiguous=True))
            if (2 * jl + 1) in layers:
                for half in range(2):
                    gqk.append(P.dma("sp", a_GQK[half * 64:(half + 1) * 64, jl, 0:3], attn_q_gain[jl].rearrange("g d -> d g"),
                                     allow_slow_non_contiguous=True))
                    gqk.append(P.dma("sp", a_GQK[half * 64:(half + 1) * 64, jl, 3:6], attn_k_gain[jl].rearrange("g d -> d g"),
                                     allow_slow_non_contiguous=True))
        bCW = Buf(); bGQK = Buf()
        P.op("pool", lambda e: e.memset(JOINT[:, 0:1], 0.0), extra=cwk, wr=[bCW])
        P.op("pool", lambda e: e.memset(JOINT[:, 1:2], 0.0), extra=gqk, wr=[bGQK])

        def norm_rows(xap, np_, hn, bx, bhn, col):
            ssc = SS[0:np_, col:col + 1]
            rsc = RSTD[0:np_, col:col + 1]
            P.act(hn[0:np_, :], xap, AF.Square, rd=[bx], wr=[bhn, bSS], accum_out=ssc)
            P.act(rsc, ssc, AF.Ln, rd=[bSS, bCONST], wr=[bRSTD], scale=1.0 / D_MODEL, bias=EPSC[0:np_, :])
            P.act(rsc, rsc, AF.Exp, rd=[bRSTD], wr=[bRSTD], scale=-0.5)
            P.stt("dve", hn[0:np_, :], xap, rsc, GB[0:np_, :], ALU.mult, ALU.mult, rd=[bx, bRSTD, bGB], wr=[bhn])

        def norm_phase(gain_dram_row, with_halo):
            gsrc = bass.AP(gain_dram_row.tensor, gain_dram_row.offset, [[0, 128], [1, D_MODEL]])
            if gb_preloaded[0]:
                gb_preloaded[0] = False
            else:
                P.dma("sp", GB[:], gsrc, wr=[bGB])
            P.memset("pool", SS[:], 0.0, wr=[bSS])
            for i in range(NT):
                hn = HN[i % 2]
                norm_rows(X[:, i, :], 128, hn, bX[i], bHN[i % 2], i)
                for c in range(8):
                    P.tr(PST[:, c, :], hn[:, c * 128:(c + 1) * 128], IDENT[:], rd=[bHN[i % 2], bCONST], wr=[bPST], sig=(c == 7))
                P.cp("dve", HT[:, :, i * 128:(i + 1) * 128], PST[:], rd=[bPST], wr=[bHT[i]])
            while deferred:
                deferred.pop(0)()
            if with_halo:
                norm_rows(XH[:], 2, HNH, bXH, bHNH, NT)
                for c in range(8):
                    P.tr(PST[:, c, 0:2], HNH[0:2, c * 128:(c + 1) * 128], IDENT[0:2, 0:2], rd=[bHNH, bCONST], wr=[bPST], sig=(c == 7))
                P.acopy(HTH[:], PST[:, :, 0:2], rd=[bPST], wr=[bHTH])

        def out_proj(YT, bYT, WO, bWO, nk, first_last):
            order = list(range(NT))
            if first_last:
                order = [NT - 1] + list(range(NT - 1))
            for i in order:
                for nh in range(2):
                    po, bpo = psum()
                    for kk in range(nk):
                        P.mm(po[:, :], YT[:, kk, i * 128:(i + 1) * 128], WO[:, kk, nh * 512:(nh + 1) * 512],
                             kk == 0, kk == nk - 1, rd=[bWO] + bYT, wr=[bpo], sig=(kk == nk - 1))
                    xs = X[:, i, nh * 512:(nh + 1) * 512]
                    P.tt("dve", xs, po[:, :], xs, ALU.add, rd=[bpo], wr=[bX[i]])

        def conv_layer(jl):
            winv = conv_w_in[jl].rearrange("(k p) (s j c) -> p k s j c", p=128, s=4, j=16, c=128)
            woutv = conv_w_out[jl].rearrange("(r kk p) n -> p r kk n", p=128, kk=4)
            bWJ = [Buf(), Buf()]; bWO = [Buf(), Buf()]; bYT = [Buf() for _ in range(4)]
            bVV = [Buf(), Buf()]; bCS = [Buf(), Buf()]; bSZ = [Buf(), Buf()]; bT1 = [Buf(), Buf()]
            bCH = Buf()

            def load_wj(j):
                for s in range(4):
                    P.dma("pool", c_WJ[j % 2][:, :, s, :], winv[:, :, s, j, :], wr=[bWJ[j % 2]])

            def load_wo(r):
                for kk in range(4):
                    P.dma("pool", c_WO[r % 2][:, kk, :], woutv[:, r, kk, :], wr=[bWO[r % 2]])

            load_wj(0)
            norm_phase(conv_norm[jl], True)
            it = 0
            for j in range(16):
                if j + 1 < 16:
                    load_wj(j + 1)
                if j % 4 == 0:
                    load_wo(j // 4)
                WJ = c_WJ[j % 2]
                bw = bWJ[j % 2]
                ph, bph = psum()
                for sec in (1, 2):
                    for k in range(8):
                        P.mm(ph[:, (sec - 1) * 2:(sec - 1) * 2 + 2], WJ[:, k, sec, :], HTH[:, k, :], k == 0, k == 7,
                             rd=[bw, bHTH], wr=[bph], sig=(sec == 2 and k == 7))
                P.acopy(c_CH[:], ph[:, 0:2], rd=[bph], wr=[bCH])
                for s in range(4):
                    vb = it % 2
                    it += 1
                    VV = c_VV[vb]; CS = c_CS[vb]; SZ = c_SZ[vb]; T1 = c_T1[vb]
                    pss = {}
                    for sec in (1, 2, 3, 0):
                        pq, bpq = psum()
                        pss[sec] = (pq, bpq)
                        for k in range(8):
                            P.mm(pq[:, :], WJ[:, k, sec, :], HT[:, k, s * 512:(s + 1) * 512], k == 0, k == 7,
                                 rd=[bw] + bHT[4 * s:4 * s + 4], wr=[bpq], sig=(k == 7))
                    pc, bpc = pss[1]; pu, bpu = pss[2]; pz, bpz = pss[3]; pb, bpb = pss[0]
                    P.acopy(CS[:], pc[:, :], rd=[bpc], wr=[bCS[vb]])
                    if s == 0:
                        P.tt("dve", VV[:, 0:2], c_CH[:], ph[:, 2:4], ALU.mult, rd=[bCH, bph], wr=[bVV[vb]])
                    else:
                        P.cp("dve", VV[:, 0:2], c_VV[1 - vb][:, 512:514], rd=[bVV[1 - vb]], wr=[bVV[vb]])
                    P.tt("dve", VV[:, 2:514], CS[:], pu[:, :], ALU.mult, rd=[bCS[vb], bpu], wr=[bVV[vb]])
                    P.act(SZ[:], pz[:, :], AF.Silu, rd=[bpz], wr=[bSZ[vb]])
                    P.ts("dve", T1[:], VV[:, 2:514], c_CW[:, jl, j, 2:3], ALU.mult, rd=[bVV[vb], bCW], wr=[bT1[vb]])
                    P.stt("dve", T1[:], VV[:, 1:513], c_CW[:, jl, j, 1:2], T1[:], ALU.mult, ALU.add, rd=[bVV[vb], bT1[vb]], wr=[bT1[vb]])
                    P.stt("dve", T1[:], VV[:, 0:512], c_CW[:, jl, j, 0:1], T1[:], ALU.mult, ALU.add, rd=[bVV[vb], bT1[vb]], wr=[bT1[vb]])
                    P.tt("dve", T1[:], T1[:], pb[:, :], ALU.mult, rd=[bT1[vb], bpb], wr=[bT1[vb]])
                    P.tt("pool", c_YT[:, j % 4, s * 512:(s + 1) * 512], T1[:], SZ[:], ALU.mult, rd=[bT1[vb], bSZ[vb]], wr=[bYT[j % 4]])
                if j % 4 == 3:
                    r = j // 4
                    out_proj(c_YT, bYT, c_WO[r % 2], bWO[r % 2], 4, r == 3)

        def bias_setup():
            bRB = Buf(); bOH = Buf(); bF = Buf(); bFD = Buf(); bFT = Buf()
            P.memset("pool", a_RB[:], 0.0, wr=[bRB])
            P.memset("pool", a_RB[32:33, :], 1.0, wr=[bRB])
            P.dma("sp", a_RB[0:32, :], rel_bias, wr=[bRB])
            P.dma("sp", a_OH[:], oh_in, wr=[bOH])
            for g in range(3):
                ps, bps = psum()
                P.mm(ps[0:16, 0:FPER], a_RB[:, g * 16:(g + 1) * 16], a_OH[:, g, :], True, True, rd=[bRB, bOH], wr=[bps])
                P.cp("dve", a_F[:, g, :], ps[0:16, 0:FPER], rd=[bps], wr=[bF])
            P.dma("sp", FD.ap().rearrange("g h j -> h g j"), a_F[:], rd=[bF], wr=[bFD])
            for g in range(3):
                src = bass.AP(FD, g * 16 * FPER, [[FPER, 16], [0, 128], [1, FPER]])
                dst = bass.AP(FT, g * 16 * 128 * FPER, [[128 * FPER, 16], [FPER, 128], [1, FPER]])
                P.dma("sp", dst, src, rd=[bFD], wr=[bFT])
            return bFT

        def attn_layer(jl, l, bFT):
            win = attn_w_in[jl]
            wqkv = win[:, 0:9216].rearrange("(k p) (g w h c) -> p k g w h c", p=128, g=3, w=3, h=8, c=128)
            wz = win[:, 9216:10240].rearrange("(k p) (h c) -> p k h c", p=128, c=128)
            woutv = attn_w_out[jl].rearrange("(r kk p) n -> p r kk n", p=128, kk=2)
            bWA = [Buf(), Buf()]; bWZ = [Buf()] * 2; bWO = Buf(); bYT = [Buf(), Buf()]
            bACC = Buf(); bSZ = Buf(); bQT = [Buf() for _ in range(4)]; bKT = Buf(); bVB = Buf()
            bKTH = [Buf() for _ in range(3)]; bVBH = [Buf() for _ in range(3)]
            bBT = [Buf(), Buf()]; bSQ = [Buf(), Buf()]; bRS = [Buf(), Buf()]; bE = [Buf() for _ in range(4)]
            bPT = [Buf() for _ in range(8)]
            bPTH = [Buf() for _ in range(6)]
            bR = [Buf(), Buf()]; bTMP = [Buf(), Buf()]

            def load_wa(hp, g, buf):
                for w in range(3):
                    P.dma("pool", a_WA[buf][:, :, w, :], wqkv[:, :, g, w, hp, :], wr=[bWA[buf]])

            def load_bt(hp, g, buf):
                src = bass.AP(FT, ((g * 16 + 2 * hp) * 128 * FPER) + 127, [[FPER - 1, 128], [128 * FPER, 2], [1, 256]])
                P.dma("sp", a_BT[buf][:], src, rd=[bFT], wr=[bBT[buf]])

            load_wa(0, 0, 0)
            norm_phase(attn_norm[jl], False)
            P.memset("pool", a_VB[:, :, 64:128], 1.0, wr=[bVB])
            e_next = [0]
            pending_out = [False]
            pt_next = [0]
            pth_next = [0]
            it = 0
            for hp in range(8):
                P.dma("pool", a_WZ[hp % 2][:], wz[:, :, hp, :], wr=[bWZ[hp % 2]])
                def z_chunk(s):
                    ps, bps = psum()
                    for k in range(8):
                        P.mm(ps[:, :], a_WZ[hp % 2][:, k, :], HT[:, k, s * 512:(s + 1) * 512], k == 0, k == 7,
                             rd=[bWZ[hp % 2]] + bHT[4 * s:4 * s + 4], wr=[bps], sig=(k == 7))
                    P.act(a_SZ[:, s * 512:(s + 1) * 512], ps[:, :], AF.Silu, rd=[bps], wr=[bSZ])
                z_plan = {0: (0, 1), 1: (2,), 2: (3,)}
                for g, dil in enumerate(GROUP_DIL):
                    wbuf = it % 2
                    it += 1
                    nb = 16 // dil
                    WA = a_WA[wbuf]
                    if not (hp == 7 and g == 2):
                        nhp, ng = (hp, g + 1) if g < 2 else (hp + 1, 0)
                        load_wa(nhp, ng, 1 - wbuf)
                    if hp % 2 == 0 and g == 2:
                        for kk in range(2):
                            P.dma("pool", a_WO[:, kk, :], woutv[:, hp // 2, kk, :], wr=[bWO])
                    load_bt(hp, g, wbuf)
                    BT = a_BT[wbuf]

                    def qk_proj(w, dest, bdest, gcol):
                        dvr = dest[:].rearrange("p (r s) -> p r s", r=dil)
                        n_s = 512 // dil
                        st = {}

                        def stage_p(s):
                            ps, bps = psum()
                            for k in range(8):
                                P.mm(ps[:, :], WA[:, k, w, :], HT[:, k, s * 512:(s + 1) * 512], k == 0, k == 7,
                                     rd=[bWA[wbuf]] + bHT[4 * s:4 * s + 4], wr=[bps], sig=(k == 7))
                            sb = s % 2
                            P.act(a_SQ[sb][:], ps[:, :], AF.Square, rd=[bps], wr=[bSQ[sb]])
                            st[s] = (ps, bps)

                        def stage_o(s):
                            ps, bps = st.pop(s)
                            sb = s % 2
                            ps2, bps2 = psum()
                            P.mm(ps2[:, :], ONESBD[:], a_SQ[sb][:], True, True, rd=[bSQ[sb], bCONST], wr=[bps2])
                            P.act(a_RS[sb][:], ps2[:, :], AF.Ln, rd=[bps2, bCONST], wr=[bRS[sb]], scale=1.0 / 64, bias=EPSC[:, :])
                            P.act(a_RS[sb][:], a_RS[sb][:], AF.Exp, rd=[bRS[sb]], wr=[bRS[sb]], scale=-0.5)
                            P.stt("dve", dvr[:, :, s * n_s:(s + 1) * n_s], ps[:, :].rearrange("p (s r) -> p r s", r=dil),
                                  a_GQK[:, jl, gcol:gcol + 1], a_RS[sb][:].rearrange("p (s r) -> p r s", r=dil),
                                  ALU.mult, ALU.mult, rd=[bps, bRS[sb], bGQK], wr=[bdest[s] if isinstance(bdest, list) else bdest])

                        stage_p(0)
                        for s in range(1, 4):
                            stage_p(s)
                            stage_o(s - 1)
                        return lambda: stage_o(3)

                    k_tail = qk_proj(1, a_KT, bKT, 3 + g)
                    for qd in range(4):
                        if qd == 1:
                            k_tail()
                        ps, bps = psum()
                        for bb in range(4):
                            blk = qd * 4 + bb
                            r, n = blk // nb, blk % nb
                            t0 = r + dil * 128 * n
                            for k in range(8):
                                P.mm(ps[:, bb * 128:(bb + 1) * 128], HT[:, k, t0:t0 + dil * 127 + 1:dil], WA[:, k, 2, :], k == 0, k == 7,
                                     rd=[bWA[wbuf]] + bHT, wr=[bps], sig=(bb == 3 and k == 7))
                        dst = sb_ap(a_VB, qd * 4 * 192, [[192, 4], [128, 2], [1, 64]])
                        P.acopy(dst, ps[:, :].rearrange("p (b h d) -> p b h d", b=4, h=2), rd=[bps], wr=[bVB])
                    if pending_out[0]:
                        pending_out[0] = False
                        out_proj(a_YT, bYT, a_WO, bWO, 2, False)
                    sndk, sndv = SND[(l, hp, g)]; rcvk, rcvv = RCV[(l, hp, g)]
                    bSk = Buf(); bSv = Buf(); bRk = Buf(); bRv = Buf()
                    ksrc = sb_ap(a_KT, (nb - 1) * 128, [[nb * 128, dil], [1, 128]])
                    P.dma("sp", sndk.ap().rearrange("p (r c) -> p r c", c=128), ksrc, rd=[bKT], wr=[bSk])
                    vsrc = sb_ap(a_VB, (nb - 1) * 192, [[nb * 192, dil], [1, 192]])
                    P.dma("sp", sndv.ap().rearrange("p (r c) -> p r c", c=192), vsrc, rd=[bVB], wr=[bSv])
                    P.collective(lambda e, a=sndk, b_=rcvk: e.collective_compute(
                        "AllGather", ALU.bypass, replica_groups=PAIRS, ins=[a.ap().opt()], outs=[b_.ap().opt()]),
                        rd=[bSk], wr=[bRk])
                    P.collective(lambda e, a=sndv, b_=rcvv: e.collective_compute(
                        "AllGather", ALU.bypass, replica_groups=PAIRS, ins=[a.ap().opt()], outs=[b_.ap().opt()]),
                        rd=[bSv], wr=[bRv])
                    P.dma("sp", a_KTH[g][:], rcvk.ap()[0:128, :], rd=[bRk], wr=[bKTH[g]])
                    P.dma("sp", a_VBH[g][:], rcvv.ap()[0:128, :].rearrange("p (r c) -> p r c", c=192), rd=[bRv], wr=[bVBH[g]])
                    q_tail = qk_proj(0, a_QT, bQT, g)
                    zc = z_plan[g]
                    z_chunk(zc[0])
                    q_tail()
                    for s_ in zc[1:]:
                        z_chunk(s_)
                    LA = 5
                    assert len(a_PT) >= LA + 3 and len(a_PTH) >= LA + 1

                    def score(head, kt_ap, bkt, q0, N, btc0, ptile, bpt, halo=False):
                        rows = slice(64 * head, 64 * head + 64)
                        ps, bps = psum()
                        if dil == 1:
                            qc = range(q0 // 512, (q0 + N - 1) // 512 + 1)
                        elif dil == 4:
                            n_ = (q0 // 128) % 4
                            qc = range(n_, n_ + (2 if N == 256 else 1))
                        else:
                            qc = range(4)
                        P.mm(ps[:, 0:N], kt_ap, a_QT[rows, q0:q0 + N], True, True, rd=[bkt] + [bQT[c_] for c_ in qc], wr=[bps])
                        ei = e_next[0]
                        e_next[0] = (ei + 1) % len(a_E)
                        P.stt("dve", a_E[ei][:, 0:N], ps[:, 0:N], 0.125, BT[:, head, btc0:btc0 + N], ALU.mult, ALU.add,
                              rd=[bps, bBT[wbuf]], wr=[bE[ei]])
                        if halo:
                            P.act(ptile[:, 0:N], a_E[ei][:, 0:N], AF.Exp, rd=[bE[ei], bFLAG], wr=[bpt], bias=NEGF[:, :])
                        else:
                            P.act(ptile[:, 0:N], a_E[ei][:, 0:N], AF.Exp, rd=[bE[ei]], wr=[bpt])

                    tasks = [(head, b) for b in range(16) for head in range(2)]
                    slot = {}
                    hslot = {}
                    acc_banks = {}

                    def emit_scores(head, b):
                        rows = slice(64 * head, 64 * head + 64)
                        r, n = b // nb, b % nb
                        N = 256 if n < nb - 1 else 128
                        sl = pt_next[0] % len(a_PT)
                        pt_next[0] += 1
                        slot[(head, b)] = sl
                        score(head, a_KT[rows, b * 128:(b + 1) * 128], bKT, b * 128, N, 0, a_PT[sl], bPT[sl])
                        if n == 0 and not NOHALO:
                            hs = pth_next[0] % len(a_PTH)
                            pth_next[0] += 1
                            hslot[(head, b)] = hs
                            score(head, a_KTH[g][rows, r * 128:(r + 1) * 128], bKTH[g], b * 128, 128, 128, a_PTH[hs], bPTH[hs], halo=True)

                    def emit_pv(head, b):
                        vc = slice(64 * head, 64 * head + 128)
                        r, n = b // nb, b % nb
                        qd, bb = b // 4, b % 4
                        if bb == 0:
                            acc_banks[(head, qd)] = psum_acc()
                        po, bpo = acc_banks[(head, qd)]
                        pob = po[:, bb * 128:(bb + 1) * 128]
                        sl = slot[(head, b)]
                        if n == 0 and NOHALO:
                            pass
                        elif n == 0:
                            hs = hslot[(head, b)]
                            P.mm(pob, a_VBH[g][:, r, vc], a_PTH[hs][:, 0:128], True, False, rd=[bVBH[g], bPTH[hs]], wr=[bpo], sig=False)
                        else:
                            sp_ = slot[(head, b - 1)]
                            P.mm(pob, a_VB[:, b - 1, vc], a_PT[sp_][:, 128:256], True, False, rd=[bVB, bPT[sp_]], wr=[bpo], sig=False)
                        P.mm(pob, a_VB[:, b, vc], a_PT[sl][:, 0:128], (n == 0 and NOHALO), True, rd=[bVB, bPT[sl]], wr=[bpo], sig=(bb == 3))
                        if bb == 3:
                            b0 = qd * 4
                            if dil == 1:
                                acc = sb_ap(a_ACC, head * T + b0 * 128, [[1, 512]])
                                pin = po[:, :]
                            elif dil == 4:
                                acc = sb_ap(a_ACC, head * T + qd, [[4, 512]])
                                pin = po[:, :]
                            else:
                                acc = sb_ap(a_ACC, head * T + b0, [[1, 4], [16, 128]])
                                pin = po[:, :].rearrange("p (b i) -> p b i", b=4)
                            if g == 0:
                                P.cp("dve", acc, pin, rd=[bpo], wr=[bACC])
                            else:
                                P.tt("dve", acc, pin, acc, ALU.add, rd=[bpo], wr=[bACC])

                    for i in range(len(tasks) + LA):
                        if i < len(tasks):
                            emit_scores(*tasks[i])
                        if i >= LA:
                            emit_pv(*tasks[i - LA])
                for c in range(8):
                    cb = c % 2
                    cs = slice(c * 256, (c + 1) * 256)
                    P.act(a_R[cb][0:64, :], a_ACC[64:128, 0, cs], AF.Ln, rd=[bACC], wr=[bR[cb]])
                    P.act(a_R[cb][64:128, :], a_ACC[0:64, 1, cs], AF.Ln, rd=[bACC], wr=[bR[cb]])
                    P.act(a_R[cb][:, :], a_R[cb][:, :], AF.Exp, rd=[bR[cb]], wr=[bR[cb]], scale=-1.0)
                    P.tt("dve", a_TMP[cb][0:64, :], a_ACC[0:64, 0, cs], a_R[cb][0:64, :], ALU.mult, rd=[bACC, bR[cb]], wr=[bTMP[cb]])
                    P.tt("dve", a_TMP[cb][64:128, :], a_ACC[64:128, 1, cs], a_R[cb][64:128, :], ALU.mult, rd=[bACC, bR[cb]], wr=[bTMP[cb]])
                    P.tt("pool", a_YT[:, hp % 2, cs], a_TMP[cb][:], a_SZ[:, cs], ALU.mult, rd=[bTMP[cb], bSZ], wr=[bYT[hp % 2]])
                if hp % 2 == 1:
                    if hp == 7:
                        out_proj(a_YT, bYT, a_WO, bWO, 2, True)
                    else:
                        pending_out[0] = True

        bFT_box = [None]
        deferred = []
        if has_attn:
            if layers[0] % 2 == 1:
                bFT_box[0] = bias_setup()
                P.barrier()
            else:
                deferred.append(lambda: bFT_box.__setitem__(0, bias_setup()))
        for li, l in enumerate(layers):
            if li > 0:
                P.barrier()
            if l % 2 == 0:
                if li > 0:
                    bSX = Buf(); bRX = Buf()
                    P.dma("sp", SNDX.ap(), X[126:128, NT - 1, :], rd=[bX[NT - 1]], wr=[bSX])
                    P.collective(lambda e: e.collective_compute("AllGather", ALU.bypass, replica_groups=PAIRS,
                                                                ins=[SNDX.ap().opt()], outs=[RCVX.ap().opt()]), rd=[bSX], wr=[bRX])
                    P.dma("sp", XH[:], RCVX.ap()[0:2, :], rd=[bRX], wr=[bXH])
                    P.ts("dve", XH[:], XH[:], FLAG[0:2, 0:1], ALU.mult, rd=[bFLAG], wr=[bXH])
                ps_pool[0] = list(range(7))
                conv_layer(l // 2)
            else:
                ps_pool[0] = [0, 1, 2, 3, 4, 7]
                attn_layer(l // 2, l, bFT_box[0])

        yv = y_out.rearrange("(i p) d -> p i d", p=128)
        tks = []
        for i in range(NT):
            tks.append(P.dma("sp", yv[:, i, :], X[:, i, :], rd=[bX[i]]))
        P.wait_all("sp", tks)
        P.emit(block)
    return nc


_CONST = {}


def _consts():
    if not _CONST:
        bd = np.zeros((128, 128), np.float32)
        bd[0:64, 0:64] = 1.0
        bd[64:128, 64:128] = 1.0
        _CONST["ident"] = np.eye(128, dtype=np.float32)
        _CONST["onesbd"] = bd
        _CONST["oh"] = bias_onehot()
    return _CONST


_PROGS = {}


def _run(layers, x_shards, xh_shards, weights):
    key = tuple(layers)
    if key not in _PROGS:
        _PROGS[key] = build(list(layers))
    nc = _PROGS[key]
    c = _consts()
    in_maps = []
    for core in range(NCORES):
        m = dict(weights)
        m["x"] = x_shards[core]
        m["xh"] = xh_shards[core]
        m["flag"] = np.full((128, 1), float(core % 2), np.float32)
        m["ident"] = c["ident"]
        m["onesbd"] = c["onesbd"]
        m["oh"] = c["oh"]
        in_maps.append(m)
    res = run_bass_kernel_spmd(nc, in_maps, core_ids=list(range(NCORES)))
    return [r["y"] for r in res.results]


def _shard(x):
    xs, xh = [], []
    for core in range(NCORES):
        b, half = core // 2, core % 2
        xs.append(np.ascontiguousarray(x[b, half * T:(half + 1) * T, :]))
        if half == 0:
            xh.append(np.zeros((2, D_MODEL), np.float32))
        else:
            xh.append(np.ascontiguousarray(x[b, T - 2:T, :]))
    return xs, xh


def _unshard(ys):
    out = np.empty((4, 2 * T, D_MODEL), np.float32)
    for core in range(NCORES):
        b, half = core // 2, core % 2
        out[b, half * T:(half + 1) * T, :] = ys[core]
    return out


def kernel(x, conv_norm, conv_w_in, conv_w, conv_w_out, attn_norm, attn_w_in,
           attn_q_gain, attn_k_gain, attn_w_out, rel_bias, _layers=None):
    f = lambda a: np.ascontiguousarray(np.asarray(a, dtype=np.float32))
    weights = dict(conv_norm=f(conv_norm), conv_w_in=f(conv_w_in), conv_w=f(conv_w), conv_w_out=f(conv_w_out),
                   attn_norm=f(attn_norm), attn_w_in=f(attn_w_in), attn_q_gain=f(attn_q_gain),
                   attn_k_gain=f(attn_k_gain), attn_w_out=f(attn_w_out), rel_bias=f(rel_bias))
    x = f(x)
    if _layers is not None:
        groups = _layers
    elif FUSED:
        groups = [[0, 1, 2, 3]]
    else:
        groups = [[0], [1], [2], [3]]
    for grp in groups:
        xs, xh = _shard(x)
        ys = _run(grp, xs, xh, weights)
        x = _unshard(ys)
    return x
```
